# Optimizing a Trainium2 kernel written in Bass

```python
import jax, jax.numpy as jnp
from jax import lax
import numpy as np

D_MODEL = 1024
BATCH = 16
SEQ = 2048
DEPTH = 1
DEC_BATCH = 128
DEC_SEQ = 1
PAST_LEN = 16384
PAGE_SIZE = 128

HEAD_DIM = 64
N_HEADS_A = 16
N_KV_A = 2
GROUP_A = N_HEADS_A // N_KV_A
WINDOW_A = 128
DILATED_PAIRS = ((128, 1), (512, 4), (2048, 16))
N_GROUPS_B = 3
HEADS_PER_GROUP_B = 4
SPAN = 128
BLOCK = 128
D_FF = ((8 * D_MODEL + 2) // 3 + 255) // 256 * 256
ROPE_THETA = 10000.0
NORM_EPS = 1e-6
NEG_INF = -1e30
Q_A_WIDTH = N_HEADS_A * HEAD_DIM
KV_A_WIDTH = N_KV_A * HEAD_DIM
B_WIDTH = N_GROUPS_B * HEADS_PER_GROUP_B * HEAD_DIM
OUT_B_WIDTH = HEADS_PER_GROUP_B * HEAD_DIM
N_IN = Q_A_WIDTH + 2 * KV_A_WIDTH + 3 * B_WIDTH + 2 * D_MODEL

kernel_name = "gated_swa_sink_dilated_hybrid_step"


def _rmsnorm(x, g):
    xf = x.astype(jnp.float32)
    y = xf * lax.rsqrt(jnp.mean(xf * xf, axis=-1, keepdims=True) + NORM_EPS)
    return (y * g.astype(jnp.float32)).astype(x.dtype)


def _rope(x, pos):
    half = HEAD_DIM // 2
    inv = 1.0 / (ROPE_THETA ** (jnp.arange(half, dtype=jnp.float32) / half))
    ang = pos.astype(jnp.float32)[:, None] * inv[None, :]
    ang = jnp.concatenate([ang, ang], axis=-1)
    shape = (1, pos.shape[0]) + (1,) * (x.ndim - 3) + (HEAD_DIM,)
    cos = jnp.cos(ang).reshape(shape)
    sin = jnp.sin(ang).reshape(shape)
    xf = x.astype(jnp.float32)
    rot = jnp.concatenate([-xf[..., half:], xf[..., :half]], axis=-1)
    return (xf * cos + rot * sin).astype(x.dtype)


def _lse(s, sink):
    lse = jax.nn.logsumexp(s, axis=-1)
    if sink is not None:
        lse = jnp.logaddexp(lse, sink.astype(jnp.float32))
    return lse


def _banded_attention(q, k, v, sink):
    N, L, Hk, G, Dh = q.shape
    nb = -(-L // BLOCK)
    Lp = nb * BLOCK
    pe = Lp - L
    qb = jnp.pad(q, ((0, 0), (0, pe), (0, 0), (0, 0), (0, 0))).reshape(N, nb, BLOCK, Hk, G, Dh)
    kb = jnp.pad(k, ((0, 0), (BLOCK, pe), (0, 0), (0, 0))).reshape(N, nb + 1, BLOCK, Hk, Dh)
    vb = jnp.pad(v, ((0, 0), (BLOCK, pe), (0, 0), (0, 0))).reshape(N, nb + 1, BLOCK, Hk, Dh)
    k_band = jnp.concatenate([kb[:, :-1], kb[:, 1:]], axis=2)
    v_band = jnp.concatenate([vb[:, :-1], vb[:, 1:]], axis=2)
    s = jnp.einsum('nbqhgd,nbkhd->nbhgqk', qb, k_band, preferred_element_type=jnp.float32) * (Dh ** -0.5)
    qi = jnp.arange(BLOCK)
    ki = jnp.arange(2 * BLOCK)
    dist = BLOCK + qi[:, None] - ki[None, :]
    key_pos = jnp.arange(nb)[:, None] * BLOCK - BLOCK + ki[None, :]
    mask = ((dist >= 0) & (dist <= SPAN))[None] & (key_pos >= 0)[:, None, :]
    s = jnp.where(mask[None, :, None, None], s, NEG_INF)
    lse = _lse(s, None if sink is None else sink[:, :, None])
    p = jnp.exp(s - lse[..., None])
    o = jnp.einsum('nbhgqk,nbkhd->nbqhgd', p.astype(v.dtype), v_band)
    o = o.reshape(N, Lp, Hk, G, Dh)[:, :L]
    lse = jnp.transpose(lse, (0, 1, 4, 2, 3)).reshape(N, Lp, Hk, G)[:, :L]
    return o, lse


def _gathered_attention(q, k_all, v_all, n_past, dilation, sink):
    S = q.shape[1]
    idx = n_past + jnp.arange(S)[:, None] - dilation * jnp.arange(SPAN + 1)[None, :]
    valid = idx >= 0
    idx = jnp.maximum(idx, 0)
    kg = jnp.take(k_all, idx, axis=1)
    vg = jnp.take(v_all, idx, axis=1)
    s = jnp.einsum('nshgd,nsjhd->nshgj', q, kg, preferred_element_type=jnp.float32) * (q.shape[-1] ** -0.5)
    s = jnp.where(valid[None, :, None, None, :], s, NEG_INF)
    lse = _lse(s, sink)
    p = jnp.exp(s - lse[..., None])
    o = jnp.einsum('nshgj,nsjhd->nshgd', p.astype(v_all.dtype), vg)
    return o, lse


def _to_residues(x, d):
    B, S = x.shape[:2]
    rest = x.shape[2:]
    x = jnp.swapaxes(x.reshape((B, S // d, d) + rest), 1, 2)
    return x.reshape((B * d, S // d) + rest)


def _from_residues(x, B, d):
    L = x.shape[1]
    rest = x.shape[2:]
    x = jnp.swapaxes(x.reshape((B, d, L) + rest), 1, 2)
    return x.reshape((B, L * d) + rest)


def _merge_dilated(outs, lses, dtype):
    alpha = jax.nn.softmax(jnp.stack(lses, axis=0), axis=0)
    o = jnp.einsum('gbshd,gbsh->bshd', jnp.stack(outs, axis=0).astype(jnp.float32), alpha)
    return o.astype(dtype)


def _project(x, pos, norm1_g, w_in):
    B, S, _ = x.shape
    h = _rmsnorm(x, norm1_g)
    z = h @ w_in
    o1 = Q_A_WIDTH
    o2 = o1 + KV_A_WIDTH
    o3 = o2 + KV_A_WIDTH
    o4 = o3 + B_WIDTH
    o5 = o4 + B_WIDTH
    o6 = o5 + B_WIDTH
    o7 = o6 + D_MODEL
    qa = _rope(z[..., :o1].reshape(B, S, N_KV_A, GROUP_A, HEAD_DIM), pos)
    ka = _rope(z[..., o1:o2].reshape(B, S, N_KV_A, HEAD_DIM), pos)
    va = z[..., o2:o3].reshape(B, S, N_KV_A, HEAD_DIM)
    qb = _rope(z[..., o3:o4].reshape(B, S, N_GROUPS_B, HEADS_PER_GROUP_B, HEAD_DIM), pos)
    kb = _rope(z[..., o4:o5].reshape(B, S, N_GROUPS_B, HEADS_PER_GROUP_B, HEAD_DIM), pos)
    vb = z[..., o5:o6].reshape(B, S, N_GROUPS_B, HEADS_PER_GROUP_B, HEAD_DIM)
    gate_a = jax.nn.sigmoid(z[..., o6:o7])
    gate_b = jax.nn.sigmoid(z[..., o7:])
    return qa, ka, va, qb, kb, vb, gate_a, gate_b


def _finish(x, oa, ob, gate_a, gate_b, w_pa, w_pb, w_o, norm2_g, w_gu, w_down):
    B, S, _ = x.shape
    m = gate_a * (oa.reshape(B, S, Q_A_WIDTH) @ w_pa) + gate_b * (ob.reshape(B, S, OUT_B_WIDTH) @ w_pb)
    x = x + m @ w_o
    gu = _rmsnorm(x, norm2_g) @ w_gu
    return x + (jax.nn.silu(gu[..., :D_FF]) * gu[..., D_FF:]) @ w_down


def _prompt_layer(x, norm1_g, w_in, sinks, w_pa, w_pb, w_o, norm2_g, w_gu, w_down):
    B, S, _ = x.shape
    pos = jnp.arange(S, dtype=jnp.int32)
    qa, ka, va, qb, kb, vb, gate_a, gate_b = _project(x, pos, norm1_g, w_in)
    oa, _ = _banded_attention(qa, ka, va, sinks.reshape(N_KV_A, GROUP_A))
    rows_a = min(WINDOW_A, S)
    state_a = jnp.stack([ka[:, S - rows_a:], va[:, S - rows_a:]], axis=2)
    outs, lses, states_b = [], [], []
    for gi, (win, dil) in enumerate(DILATED_PAIRS):
        q = _to_residues(qb[:, :, gi], dil)[:, :, :, None, :]
        o, lse = _banded_attention(q, _to_residues(kb[:, :, gi], dil), _to_residues(vb[:, :, gi], dil), None)
        outs.append(_from_residues(o[:, :, :, 0], B, dil))
        lses.append(_from_residues(lse[..., 0], B, dil))
        rows = min(win, S)
        states_b.append(jnp.stack([kb[:, S - rows:, gi], vb[:, S - rows:, gi]], axis=2))
    ob = _merge_dilated(outs, lses, x.dtype)
    y = _finish(x, oa, ob, gate_a, gate_b, w_pa, w_pb, w_o, norm2_g, w_gu, w_down)
    return y, state_a, states_b


def _sample_layer(x, cache_a, caches_b, norm1_g, w_in, sinks, w_pa, w_pb, w_o, norm2_g, w_gu, w_down):
    B, S, _ = x.shape
    pos = PAST_LEN + jnp.arange(S, dtype=jnp.int32)
    qa, ka, va, qb, kb, vb, gate_a, gate_b = _project(x, pos, norm1_g, w_in)
    k_all = jnp.concatenate([cache_a[:, :, 0], ka], axis=1)
    v_all = jnp.concatenate([cache_a[:, :, 1], va], axis=1)
    oa, _ = _gathered_attention(qa, k_all, v_all, cache_a.shape[1], 1, sinks.reshape(N_KV_A, GROUP_A))
    rows_a = min(WINDOW_A, PAST_LEN + S)
    n_a = k_all.shape[1]
    state_a = jnp.stack([k_all[:, n_a - rows_a:], v_all[:, n_a - rows_a:]], axis=2)
    outs, lses, states_b = [], [], []
    for gi, (win, dil) in enumerate(DILATED_PAIRS):
        cb = caches_b[gi]
        kg_all = jnp.concatenate([cb[:, :, 0], kb[:, :, gi]], axis=1)
        vg_all = jnp.concatenate([cb[:, :, 1], vb[:, :, gi]], axis=1)
        o, lse = _gathered_attention(qb[:, :, gi][:, :, :, None, :], kg_all, vg_all, cb.shape[1], dil, None)
        outs.append(o[:, :, :, 0])
        lses.append(lse[..., 0])
        rows = min(win, PAST_LEN + S)
        n_g = kg_all.shape[1]
        states_b.append(jnp.stack([kg_all[:, n_g - rows:], vg_all[:, n_g - rows:]], axis=2))
    ob = _merge_dilated(outs, lses, x.dtype)
    y = _finish(x, oa, ob, gate_a, gate_b, w_pa, w_pb, w_o, norm2_g, w_gu, w_down)
    return y, state_a, states_b


def setup_inputs(seed: int = 0) -> dict:
    key = jax.random.key(seed)
    ks = jax.random.split(key, 16)
    f32 = jnp.float32
    la = min(WINDOW_A, PAST_LEN)
    lb = [min(w, PAST_LEN) for (w, _) in DILATED_PAIRS]
    nrm = jax.random.normal
    return {
        "x_prompt": nrm(ks[0], (BATCH, SEQ, D_MODEL), f32),
        "x_sample": nrm(ks[1], (DEC_BATCH, DEC_SEQ, D_MODEL), f32),
        "cache_a_kv": nrm(ks[2], (DEPTH, DEC_BATCH, la, 2, N_KV_A, HEAD_DIM), f32),
        "cache_b1_kv": nrm(ks[3], (DEPTH, DEC_BATCH, lb[0], 2, HEADS_PER_GROUP_B, HEAD_DIM), f32),
        "cache_b2_kv": nrm(ks[4], (DEPTH, DEC_BATCH, lb[1], 2, HEADS_PER_GROUP_B, HEAD_DIM), f32),
        "cache_b3_kv": nrm(ks[5], (DEPTH, DEC_BATCH, lb[2], 2, HEADS_PER_GROUP_B, HEAD_DIM), f32),
        "norm1_g": 1.0 + 0.02 * nrm(ks[6], (DEPTH, D_MODEL), f32),
        "w_in": nrm(ks[7], (DEPTH, D_MODEL, N_IN), f32) * D_MODEL ** -0.5,
        "sinks": 0.5 * nrm(ks[8], (DEPTH, N_HEADS_A), f32),
        "w_pa": nrm(ks[9], (DEPTH, Q_A_WIDTH, D_MODEL), f32) * Q_A_WIDTH ** -0.5,
        "w_pb": nrm(ks[10], (DEPTH, OUT_B_WIDTH, D_MODEL), f32) * OUT_B_WIDTH ** -0.5,
        "w_o": nrm(ks[11], (DEPTH, D_MODEL, D_MODEL), f32) * D_MODEL ** -0.5,
        "norm2_g": 1.0 + 0.02 * nrm(ks[12], (DEPTH, D_MODEL), f32),
        "w_gu": nrm(ks[13], (DEPTH, D_MODEL, 2 * D_FF), f32) * D_MODEL ** -0.5,
        "w_down": nrm(ks[14], (DEPTH, D_FF, D_MODEL), f32) * D_FF ** -0.5,
        "final_norm_g": 1.0 + 0.02 * nrm(ks[15], (D_MODEL,), f32),
    }


def reference(x_prompt, x_sample, cache_a_kv, cache_b1_kv, cache_b2_kv, cache_b3_kv, norm1_g, w_in, sinks, w_pa, w_pb, w_o, norm2_g, w_gu, w_down, final_norm_g):
    xp = x_prompt
    xs = x_sample
    pa, pb1, pb2, pb3, sa, sb1, sb2, sb3 = [], [], [], [], [], [], [], []
    for l in range(DEPTH):
        xp, st_a, st_b = _prompt_layer(xp, norm1_g[l], w_in[l], sinks[l], w_pa[l], w_pb[l], w_o[l], norm2_g[l], w_gu[l], w_down[l])
        pa.append(st_a)
        pb1.append(st_b[0])
        pb2.append(st_b[1])
        pb3.append(st_b[2])
        xs, st_a, st_b = _sample_layer(xs, cache_a_kv[l], (cache_b1_kv[l], cache_b2_kv[l], cache_b3_kv[l]), norm1_g[l], w_in[l], sinks[l], w_pa[l], w_pb[l], w_o[l], norm2_g[l], w_gu[l], w_down[l])
        sa.append(st_a)
        sb1.append(st_b[0])
        sb2.append(st_b[1])
        sb3.append(st_b[2])
    y_prompt = _rmsnorm(xp, final_norm_g)
    y_sample = _rmsnorm(xs, final_norm_g)
    new_pa = jnp.stack(pa)
    new_pb1 = jnp.stack(pb1)
    new_pb2 = jnp.stack(pb2)
    new_pb3 = jnp.stack(pb3)
    new_sa = jnp.stack(sa)
    new_sb1 = jnp.stack(sb1)
    new_sb2 = jnp.stack(sb2)
    new_sb3 = jnp.stack(sb3)
    return (y_prompt, y_sample, new_pa, new_pb1, new_pb2, new_pb3, new_sa, new_sb1, new_sb2, new_sb3)
```

```python
import contextlib
import numpy as np
import ml_dtypes
import concourse.bass as bass
import concourse.mybir as mybir
from concourse.bass_utils import run_bass_kernel_spmd

F32 = mybir.dt.float32
BF16 = mybir.dt.bfloat16
I32 = mybir.dt.int32
AF = mybir.ActivationFunctionType
ALU = mybir.AluOpType
AX = mybir.AxisListType

D_MODEL = 1024
SEQ = 2048
D_FF = 2816
N_IN = 5632
NEG = -30000.0
EPS = 1e-6
PAST_LEN = 16384
N_CORES = 8

ENGS = ("pe", "act", "dve", "pool", "sp")
N_DMA_SEMS = 60
N_HW_SEMS = 24
N_SW_SEMS = 28


class Prog:
    def __init__(self, nc):
        self.nc = nc
        self.ops = []

    def op(self, eng, fn, reads=(), writes=(), dma=False, final=False, bulk=False, d2d=False):
        reads = tuple(reads)
        writes = tuple(writes) + tuple(k for k in reads if k.startswith("psum") and k not in writes)
        self.ops.append(dict(eng=eng, fn=fn, reads=reads, writes=writes, dma=dma,
                             final=final, barrier=False, bulk=bulk, d2d=d2d))

    def barrier(self):
        self.ops.append(dict(eng=None, fn=None, reads=(), writes=(), dma=False, final=False,
                             barrier=True, bulk=False, d2d=False))

    def emit(self):
        nc = self.nc
        ops = self.ops
        n = len(ops)
        last_w = {}
        readers = {}
        deps = [None] * n
        for i, o in enumerate(ops):
            d = set()
            if o["barrier"]:
                deps[i] = d
                continue
            for k in o["reads"]:
                if k in last_w:
                    d.add(last_w[k])
            for k in o["writes"]:
                if k in last_w:
                    d.add(last_w[k])
                for r in readers.get(k, ()):
                    d.add(r)
            d.discard(i)
            for k in o["writes"]:
                last_w[k] = i
                readers[k] = []
            for k in o["reads"]:
                if k not in o["writes"]:
                    readers.setdefault(k, []).append(i)
            if o["eng"] == "pe":
                d = {j for j in d if not (ops[j]["eng"] == "pe" and not ops[j]["dma"])}
            deps[i] = d
        slot_prev = [None] * N_DMA_SEMS
        slot_val = [0] * N_DMA_SEMS
        nd_hw = 0
        nd_sw = 0
        nd_bulk = 0
        for i, o in enumerate(ops):
            if o["dma"]:
                if o["bulk"]:
                    s = N_HW_SEMS + N_SW_SEMS + nd_bulk % (N_DMA_SEMS - N_HW_SEMS - N_SW_SEMS)
                    nd_bulk += 1
                elif o["eng"] == "pool":
                    s = N_HW_SEMS + nd_sw % N_SW_SEMS
                    nd_sw += 1
                else:
                    s = nd_hw % N_HW_SEMS
                    nd_hw += 1
                o["slot"] = s
                if slot_prev[s] is not None:
                    deps[i].add(slot_prev[s])
                slot_prev[s] = i
                slot_val[s] += 16
                o["target"] = slot_val[s]
        needed = [False] * n
        for i in range(n):
            for j in deps[i]:
                needed[j] = True
        last_on = {e: None for e in ENGS}
        for i, o in enumerate(ops):
            if o["barrier"]:
                o["b_last"] = dict(last_on)
                for e in ENGS:
                    if last_on[e] is not None:
                        needed[last_on[e]] = True
            elif not o["dma"]:
                last_on[o["eng"]] = i
        cnt = {e: 0 for e in ENGS}
        cur_slot_target = [0] * N_DMA_SEMS
        for i, o in enumerate(ops):
            if o["barrier"]:
                o["b_cnt"] = {e: (ops[j]["count"] if j is not None else 0) for e, j in o["b_last"].items()}
                o["b_dma"] = list(cur_slot_target)
            elif o["dma"]:
                if not o["d2d"]:
                    cur_slot_target[o["slot"]] = o["target"]
            elif needed[i]:
                cnt[o["eng"]] += 1
                o["count"] = cnt[o["eng"]]
        per_eng = {e: [] for e in ENGS}
        for i, o in enumerate(ops):
            if o["barrier"]:
                for e in ENGS:
                    per_eng[e].append(i)
            else:
                per_eng[o["eng"]].append(i)
        self.stats = {e: len(per_eng[e]) for e in ENGS}
        with contextlib.ExitStack() as es:
            esem = {e: es.enter_context(nc.semaphore("s_" + e)) for e in ENGS}
            dsem = [es.enter_context(nc.semaphore("d_%d" % s)) for s in range(N_DMA_SEMS)]
            block = es.enter_context(nc.Block())
            nwaits = [0]

            def run(ename, eng):
                waited = {}

                def need(key, val):
                    if val <= 0 or waited.get(key, 0) >= val:
                        return
                    waited[key] = val
                    sem = dsem[key[1]] if key[0] == "d" else esem[key[1]]
                    eng.wait_ge(sem, val)
                    nwaits[0] += 1

                for i in per_eng[ename]:
                    o = ops[i]
                    if o["barrier"]:
                        for e in ENGS:
                            need(("e", e), o["b_cnt"][e])
                        for s in range(N_DMA_SEMS):
                            need(("d", s), o["b_dma"][s])
                        continue
                    reqs = {}
                    for j in deps[i]:
                        a = ops[j]
                        if a["dma"]:
                            key = ("d", a["slot"]); val = a["target"]
                        else:
                            key = ("e", a["eng"]); val = a["count"]
                        if reqs.get(key, 0) < val:
                            reqs[key] = val
                    for key, val in sorted(reqs.items()):
                        need(key, val)
                    ins = o["fn"](eng)
                    if o["dma"]:
                        ins.then_inc(dsem[o["slot"]], 16)
                    elif needed[i]:
                        ins.then_inc(esem[ename], 1)
                for i in per_eng[ename]:
                    o = ops[i]
                    if o["dma"] and o["final"]:
                        need(("d", o["slot"]), o["target"])

            @block.tensor
            def _(eng):
                run("pe", eng)

            @block.scalar
            def _(eng):
                run("act", eng)

            @block.vector
            def _(eng):
                run("dve", eng)

            @block.gpsimd
            def _(eng):
                run("pool", eng)

            @block.sync
            def _(eng):
                run("sp", eng)
            self.stats["waits"] = nwaits[0]


class Arena:
    def __init__(self, nc, es, nbytes):
        self.t = es.enter_context(nc.sbuf_tensor("arena", [128, nbytes // 2], BF16))
        self.off = 0
        self.peak = 0
        self.cap = nbytes

    def alloc(self, nelem, dt):
        nb = nelem * (4 if dt in (F32, I32) else 2)
        nb = (nb + 63) // 64 * 64
        s = self.off
        self.off += nb
        self.peak = max(self.peak, self.off)
        assert self.off <= self.cap, ("arena overflow", self.off, self.cap)
        ap = self.t[:, s // 2:(s + nb) // 2]
        if dt in (F32, I32):
            ap = ap.bitcast(dt)
        return ap[:, 0:nelem]


def perm2(ap2, d):
    if d == 1:
        return ap2[:, None, :]
    return ap2.rearrange("p (i r) -> p r i", r=d)


def perm_tile(ap2, d, t):
    if d == 1:
        return ap2[:, t * 512:(t + 1) * 512]
    v = perm2(ap2, d)
    if d == 4:
        return v[:, t, :]
    return v[:, 4 * t:4 * t + 4, :]


def perm_block(ap2, d, bp):
    nb = 16 // d
    r, jb = bp // nb, bp % nb
    if d == 1:
        return ap2[:, jb * 128:(jb + 1) * 128]
    return perm2(ap2, d)[:, r, jb * 128:(jb + 1) * 128]


SETS = [
    dict(name="B1", d=1, nq=2, nk=2, H=4, base=1280, sink=False, gi=0, rows=128),
    dict(name="B2", d=4, nq=2, nk=2, H=4, base=1280 + 768, sink=False, gi=1, rows=512),
    dict(name="B3", d=16, nq=2, nk=2, H=4, base=1280 + 1536, sink=False, gi=2, rows=2048),
    dict(name="A", d=1, nq=8, nk=1, H=2, base=0, sink=True, gi=-1, rows=128),
]
GATE_COL = 3584


def build_program(n_seq=2, n_samp=16, do_prompt=True, do_sample=True, stages="123", dbg=None, sets=None):
    nc = bass.Bass("TRN2", target_bir_lowering=False)

    def dram(name, shape, dt, kind="Internal"):
        return nc.dram_tensor(name, list(shape), dt, kind=kind).ap()

    IN, OUT = "ExternalInput", "ExternalOutput"
    xp = dram("xp", [n_seq, SEQ, D_MODEL], F32, IN)
    w_in_d = dram("w_in", [D_MODEL, N_IN], F32, IN)
    w_pa_d = dram("w_pa", [1024, 1024], F32, IN)
    w_pb_d = dram("w_pb", [256, 1024], F32, IN)
    w_o_d = dram("w_o", [1024, 1024], F32, IN)
    w_gu_d = dram("w_gu", [1024, 2 * D_FF], F32, IN)
    w_dn_d = dram("w_dn", [D_FF, 1024], F32, IN)
    g1_d = dram("g1", [1, 1024], F32, IN)
    g2_d = dram("g2", [1, 1024], F32, IN)
    gf_d = dram("gf", [1, 1024], F32, IN)
    sinks_d = dram("sinks", [1, 16], F32, IN)
    c_ident_bf = dram("c_ident_bf", [128, 128], BF16, IN)
    c_ident_f = dram("c_ident_f", [128, 128], F32, IN)
    c_mask = dram("c_mask", [128, 512], BF16, IN)
    c_rot = dram("c_rot", [128, 128], BF16, IN)
    c_cos = dram("c_cos", [128, SEQ], F32, IN)
    c_sin = dram("c_sin", [128, SEQ], F32, IN)
    c_sinkl = dram("c_sinkl", [1, 128], BF16, IN)

    yp = dram("yp", [n_seq, SEQ, D_MODEL], F32, OUT)
    pa_o = dram("pa", [n_seq, 128, 2, 2, 64], F32, OUT)
    NS = n_samp
    xs_d = dram("xs", [NS, D_MODEL], F32, IN)
    ca_d = dram("ca", [NS, 128, 2, 2, 64], F32, IN)
    cb_d = [dram("cb1", [NS, 128, 2, 4, 64], F32, IN), dram("cb2", [NS, 512, 2, 4, 64], F32, IN),
            dram("cb3", [NS, 2048, 2, 4, 64], F32, IN)]
    c_sel = dram("c_sel", [16, 16 * 128], BF16, IN)
    c_eye16 = dram("c_eye16", [16, 16], F32, IN)
    c_cs = dram("c_cs", [16, 128], F32, IN)
    ys_o = dram("ys", [NS, D_MODEL], F32, OUT)
    sa_o = dram("sa", [NS, 128, 2, 2, 64], F32, OUT)
    sb_o = [dram("sb1", [NS, 128, 2, 4, 64], F32, OUT), dram("sb2", [NS, 512, 2, 4, 64], F32, OUT),
            dram("sb3", [NS, 2048, 2, 4, 64], F32, OUT)]
    pb_o = [dram("pb1", [n_seq, 128, 2, 4, 64], F32, OUT),
            dram("pb2", [n_seq, 512, 2, 4, 64], F32, OUT),
            dram("pb3", [n_seq, 2048, 2, 4, 64], F32, OUT)]

    w_in_b = dram("w_in_b", [D_MODEL, N_IN], BF16)
    w_pa_b = dram("w_pa_b", [1024, 1024], BF16)
    w_pb_b = dram("w_pb_b", [256, 1024], BF16)
    w_o_b = dram("w_o_b", [1024, 1024], BF16)
    w_gu_b = dram("w_gu_b", [1024, 2 * D_FF], BF16)
    w_dn_b = dram("w_dn_b", [D_FF, 1024], BF16)

    P = Prog(nc)
    with contextlib.ExitStack() as es:
        A = Arena(nc, es, 207 * 1024)
        pbank = [es.enter_context(nc.psum_tensor("pbank%d" % i, [128, 512], F32)) for i in range(8)]

        def PB(i):
            return pbank[i][:]

        def PK(i):
            return "psum%d" % i

        ident_bf = A.alloc(128, BF16)
        ident_f = A.alloc(128, F32)
        maskT = A.alloc(512, BF16)
        rotT = A.alloc(128, BF16)
        g1b = A.alloc(1024, F32)
        g2b = A.alloc(1024, F32)
        gfb = A.alloc(1024, F32)
        sinkl = A.alloc(128, BF16)
        es16 = A.alloc(16, F32)
        esrow = A.alloc(16 * 128, BF16)
        epst = A.alloc(1, F32)
        stats = A.alloc(64, F32)
        junk = A.alloc(1024, BF16)
        NW = 4
        wbuf = [A.alloc(8 * 512, BF16).rearrange("p (c n) -> p c n", c=8) for _ in range(NW)]
        m0 = A.off
        bufX = A.alloc(8192, F32)
        obacc = bufX.rearrange("p (s t) -> p s t", s=4)
        oaT = bufX.bitcast(BF16).rearrange("p (c t) -> p c t", c=8)
        obT = A.alloc(2 * SEQ, BF16).rearrange("p (c t) -> p c t", c=2)
        m1 = A.off
        hT = A.alloc(8 * SEQ, BF16).rearrange("p (c t) -> p c t", c=8)
        e1 = A.off
        A.off = m1
        gatesT = A.alloc(16 * 512, BF16).rearrange("p (c t) -> p c t", c=16)
        mT = A.alloc(8 * 512, BF16).rearrange("p (c t) -> p c t", c=8)
        actT = A.alloc(8 * 512, BF16).rearrange("p (c t) -> p c t", c=8)
        A.off = max(A.off, e1)
        m3 = A.off
        cosT = A.alloc(SEQ, F32)
        sinT = A.alloc(SEQ, F32)
        zb = [A.alloc(512, BF16) for _ in range(2)]
        t1 = [A.alloc(512, F32) for _ in range(2)]
        t2 = [A.alloc(512, F32) for _ in range(2)]
        ksum = [A.alloc(512, F32) for _ in range(2)]
        rden = [A.alloc(512, F32) for _ in range(2)]
        KT = A.alloc(2 * SEQ, BF16).rearrange("p (c t) -> p c t", c=2)
        Vaug = A.alloc(16 * 4 * 128, BF16).rearrange("p (b h e) -> p b h e", b=16, h=4)
        PT = [A.alloc(512, BF16) for _ in range(4)]
        stgV = [A.alloc(256, F32) for _ in range(2)]
        m3b = A.off
        QT = [A.alloc(4 * 512, BF16).rearrange("p (c t) -> p c t", c=4) for _ in range(2)]
        stgK = A.alloc(4 * 256, F32).rearrange("p (j f) -> p j f", j=4)
        e3b = A.off
        A.off = m3
        x1t = [A.alloc(1024, F32) for _ in range(4)]
        hn1 = [A.alloc(1024, BF16) for _ in range(4)]
        e3 = max(A.off, e3b)
        A.off = m3
        xts = [A.alloc(4 * 1024, F32).rearrange("p (j f) -> p j f", j=4) for _ in range(2)]
        hn = [A.alloc(1024, BF16) for _ in range(4)]
        aT = A.alloc(22 * 512, BF16).rearrange("p (c t) -> p c t", c=22)
        sgt = [A.alloc(512, F32) for _ in range(2)]
        tmpA = [A.alloc(512, F32) for _ in range(2)]
        tmpB = [A.alloc(512, F32) for _ in range(2)]
        yst = [A.alloc(1024, F32) for _ in range(2)]
        A.off = max(A.off, e3)
        e_all = A.off
        A.off = m0
        S_xs = A.alloc(1024, F32)
        S_hn = A.alloc(1024, BF16)
        S_hT = A.alloc(8 * 16, BF16).rearrange("p (c t) -> p c t", c=8)
        S_z = A.alloc(N_IN, F32)
        S_t1 = A.alloc(1152, F32)
        S_t2 = A.alloc(1152, F32)
        S_zq = A.alloc(3584, BF16)
        S_g = A.alloc(2048, F32)
        S_new = [A.alloc(2 * 2 * 64, F32)] + [A.alloc(2 * 4 * 64, F32) for _ in range(3)]
        S_vnew = A.alloc(14 * 128, BF16).rearrange("p (h e) -> p h e", h=14)
        S_pn = A.alloc(28 * 64, F32)
        S_pnew = A.alloc(28, F32)
        S_pd = A.alloc(16 * 28, BF16).rearrange("p (b h) -> p b h", b=16)
        S_sel = A.alloc(16 * 128, BF16).rearrange("p (b m) -> p b m", b=16)
        S_eye = A.alloc(16, F32)
        S_cs = A.alloc(128, F32)
        S_kv = [[A.alloc(256 if i == 0 else 512, F32) for i in range(4)] for _ in range(2)]
        S_va = [[A.alloc(4 * 128, BF16).rearrange("p (h e) -> p h e", h=4) for i in range(4)] for _ in range(2)]
        S_prod = A.alloc(1024, F32)
        S_st = A.alloc(32, F32)
        S_pt = [A.alloc(32, BF16) for _ in range(2)]
        S_rden = A.alloc(256, F32)
        S_oaT = A.alloc(8 * 16, BF16).rearrange("p (c t) -> p c t", c=8)
        S_obT = A.alloc(2 * 16, BF16).rearrange("p (c t) -> p c t", c=2)
        S_pa = A.alloc(1024, F32)
        S_pb = A.alloc(1024, F32)
        S_mbf = A.alloc(1024, BF16)
        S_mT = A.alloc(8 * 16, BF16).rearrange("p (c t) -> p c t", c=8)
        S_abf = A.alloc(D_FF, BF16)
        S_aT = A.alloc(22 * 16, BF16).rearrange("p (c t) -> p c t", c=22)
        S_y = A.alloc(1024, F32)
        assert A.off <= e_all, ("sample phase overflows aliased region", A.off, e_all)
        A.off = e_all
        print("arena bytes used per partition:", A.off, "peak", A.peak)

        rr = {}

        def rot(name, n):
            v = rr.get(name, 0)
            rr[name] = v + 1
            return v % n

        def dma(q, out, in_, reads, writes, final=False, bulk=False, d2d=False):
            if bulk:
                P.op(q, lambda e: e.dma_start(out=out, in_=in_, max_dma_last_dim=8192), reads=reads, writes=writes,
                     dma=True, final=final, bulk=bulk, d2d=True)
                return
            P.op(q, lambda e: e.dma_start(out=out, in_=in_), reads=reads, writes=writes, dma=True,
                 final=final, bulk=bulk, d2d=(d2d or bulk))

        for (dst, src, key) in [(ident_bf, c_ident_bf, "ident_bf"), (ident_f, c_ident_f, "ident_f"),
                                (maskT, c_mask, "maskT"), (rotT, c_rot, "rotT")]:
            dma("sp", dst, src, [], [key])
        for (dst, src, key) in [(g1b, g1_d, "g1b"), (g2b, g2_d, "g2b"), (gfb, gf_d, "gfb")]:
            dma("sp", dst, src.broadcast_to([128, 1024]), [], [key])
        dma("sp", sinkl[0:1, :], c_sinkl, [], ["sinkl"])
        dma("sp", es16[0:1, :], sinks_d, [], ["es16"])
        P.op("act", lambda e: e.activation(out=es16[0:1, :], in_=es16[0:1, :], func=AF.Exp),
             reads=["es16"], writes=["es16"])
        P.op("dve", lambda e: e.tensor_copy(esrow[0:1, :].rearrange("p (h n) -> p h n", h=16),
                                            es16[0:1, :][:, :, None].broadcast_to([1, 16, 128])),
             reads=["es16"], writes=["esrow"])
        P.op("pool", lambda e: e.memset(epst, EPS), writes=["epst"])

        wkeys = {}
        W_IN_BLOCKS = [(1280, 2048), (2048, 2816), (2816, 3584), (0, 1280), (3584, 4608), (4608, 5632)]

        def cast_w_in():
            prev = []
            for bi, (c0, c1) in enumerate(W_IN_BLOCKS):
                k2 = "w_in_b_%d" % c0
                deps_ = []
                if bi in (1, 2):
                    deps_ = ["w_in_b_%d" % W_IN_BLOCKS[0][0]]
                elif bi == 3:
                    deps_ = ["w_in_b_%d" % W_IN_BLOCKS[1][0], "w_in_b_%d" % W_IN_BLOCKS[2][0]]
                dma("pool", w_in_b[:, c0:c1], w_in_d[:, c0:c1], deps_, [k2], d2d=True)
            wkeys["w_in_b"] = None

        def cast_w(src, dst, key):
            ncols = src.shape[1]
            c = 1408 if ncols == 5632 else 1024
            s2 = src.rearrange("k (a c) -> (k a) c", c=c)
            d2 = dst.rearrange("k (a c) -> (k a) c", c=c)
            nrows = s2.shape[0]
            step = 1024
            keys = []
            for r0 in range(0, nrows, step):
                r1 = min(nrows, r0 + step)
                k2 = "%s_%d" % (key, r0)
                dma("pool", d2[r0:r1, :], s2[r0:r1, :], [], [k2], d2d=True)
                keys.append(k2)
            wkeys[key] = keys

        cast_w_in()

        def cast_rest():
            if "w_pa_b" in wkeys:
                return
            cast_w(w_pa_d, w_pa_b, "w_pa_b")
            cast_w(w_pb_d, w_pb_b, "w_pb_b")
            cast_w(w_o_d, w_o_b, "w_o_b")
            cast_w(w_gu_d, w_gu_b, "w_gu_b")
            cast_w(w_dn_d, w_dn_b, "w_dn_b")

        cast_rest()
        prefetched = {}

        def load_w_c(tag, *args):
            if tag in prefetched:
                return prefetched.pop(tag)
            return load_w(*args)

        def prefetch_w(tag, *args):
            prefetched[tag] = load_w(*args)

        def wdeps(key, c0, ncols):
            if key == "w_in_b":
                return ["w_in_b_%d" % a for (a, b) in W_IN_BLOCKS if a < c0 + ncols and c0 < b]
            return wkeys[key]

        def load_w(src_b, key, r0, nrows, c0, ncols):
            i = rot("wbuf", NW)
            kc = nrows // 128
            src = src_b[r0:r0 + nrows, c0:c0 + ncols].rearrange("(c p) n -> p c n", p=128)
            dma("sp", wbuf[i][:, 0:kc, 0:ncols], src, wdeps(key, c0, ncols), ["wbuf%d" % i])
            return wbuf[i], "wbuf%d" % i

        def rms_norm(x_ap, x_keys, gb, gkey, out_ap, out_keys, npart=128):
            si = rot("stats", 16)
            ss = stats[0:npart, 4 * si:4 * si + 1]
            ms = stats[0:npart, 4 * si + 1:4 * si + 2]
            rs = stats[0:npart, 4 * si + 2:4 * si + 3]
            sk = "stats%d" % si
            P.op("act", lambda e: e.activation(out=junk[0:npart, :], in_=x_ap, func=AF.Square,
                                               accum_out=ss),
                 reads=x_keys, writes=["junk", sk])
            P.op("act", lambda e: e.activation(out=ms, in_=ss, func=AF.Sqrt, bias=epst[0:npart, :],
                                               scale=1.0 / D_MODEL), reads=[sk, "epst"], writes=[sk])
            P.op("dve", lambda e: e.reciprocal(rs, ms), reads=[sk], writes=[sk])
            P.op("dve", lambda e: e.scalar_tensor_tensor(out=out_ap, in0=x_ap, scalar=rs, in1=gb[0:npart, :],
                                                         op0=ALU.mult, op1=ALU.mult),
                 reads=list(x_keys) + [sk, gkey], writes=out_keys)

        def transpose_to(hn_ap, hn_key, dst3, dst_key, col0, ncols=128, npart=128, bank=4):
            bk = bank
            pv = PB(bk).bitcast(BF16).rearrange("p (c t) -> p c t", c=8)
            for c in range(8):
                P.op("pe", lambda e, c=c: e.transpose(pv[:, c, 0:npart], hn_ap[:, c * 128:(c + 1) * 128],
                                                      ident_bf[0:npart, 0:npart]),
                     reads=[hn_key, "ident_bf"], writes=[PK(bk)])
            P.op("act", lambda e: e.copy(dst3[:, 0:8, col0:col0 + npart], pv[:, :, 0:npart]),
                 reads=[PK(bk)], writes=[dst_key])

        def rope_parts(bk, cos_ap, sin_ap, out_bf, out_key, ks=None, ks_key=None, shape4=False, after=None):
            i = rot("rope", 2)

            def v(ap):
                return ap.rearrange("p (r i) -> p r i", r=4) if shape4 else ap

            def part1():
                P.op("act", lambda e: e.copy(zb[i], PB(bk)), reads=[PK(bk)], writes=["zb%d" % i])
                P.op("dve", lambda e: e.tensor_tensor(v(t1[i]), v(PB(bk)), cos_ap, op=ALU.mult),
                     reads=[PK(bk), "cosT"], writes=["t1_%d" % i])

            def part2():
                P.op("pe", lambda e: e.matmul(PB(2), rotT, zb[i], start=True, stop=True),
                     reads=["rotT", "zb%d" % i], writes=[PK(2)])
                P.op("dve", lambda e: e.tensor_tensor(v(t2[i]), v(PB(2)), sin_ap, op=ALU.mult),
                     reads=[PK(2), "sinT"], writes=["t2_%d" % i])
                if ks is None:
                    P.op("pool", lambda e: e.tensor_tensor(out_bf, t1[i], t2[i], op=ALU.add),
                         reads=["t1_%d" % i, "t2_%d" % i], writes=[out_key])
                else:
                    P.op("pool", lambda e: e.tensor_tensor(ks, t1[i], t2[i], op=ALU.add),
                         reads=["t1_%d" % i, "t2_%d" % i], writes=[ks_key])
                    P.op("act", lambda e: e.copy(out_bf, ks), reads=[ks_key], writes=[out_key])
                if after is not None:
                    after()
            return part1, part2

        def stage1(s):
            def nrm(b):
                i = b % 4
                dma("sp", x1t[i], xp[s, b * 128:(b + 1) * 128, :], [], ["x1t%d" % i])
                rms_norm(x1t[i], ["x1t%d" % i], g1b, "g1b", hn1[i], ["hn1_%d" % i])
            for b in range(3):
                nrm(b)
            for b in range(16):
                if b + 3 < 16:
                    nrm(b + 3)
                transpose_to(hn1[b % 4], "hn1_%d" % (b % 4), hT, "hT%d" % b, b * 128, bank=3 + (b % 2))
                if b == 13:
                    st0 = SETS[0]
                    b0 = st0["base"]
                    prefetch_w((s, st0["name"], "k"), w_in_b, "w_in_b", 0, 1024, b0 + st0["nq"] * 128, st0["nk"] * 128)
                    prefetch_w((s, st0["name"], "v"), w_in_b, "w_in_b", 0, 1024,
                               b0 + st0["nq"] * 128 + st0["nk"] * 128, st0["H"] * 64)
                    prefetch_w((s, st0["name"], "q", 0), w_in_b, "w_in_b", 0, 1024, b0, min(512, st0["nq"] * 128))

        HTK = ["hT%d" % b_ for b_ in range(16)]
        deferred_final = []

        def stage2(s):
            dma("sp", cosT, c_cos, [], ["cosT"])
            dma("sp", sinT, c_sin, [], ["sinT"])
            P.op("dve", lambda e: e.memset(Vaug[:, :, :, 64:128], 1.0), writes=["Vaug_%d" % b_ for b_ in range(16)])
            def do_set(st):
                d, nq, nk, H, base, name = st["d"], st["nq"], st["nk"], st["H"], st["base"], st["name"]
                nb = 16 // d
                qcol, kcol, vcol = base, base + nq * 128, base + nq * 128 + nk * 128
                ncv = H * 64
                is_a = st["sink"]
                rows = st["rows"]
                out_t = pa_o if is_a else pb_o[st["gi"]]
                mark("  set " + name)

                def state_block(bp):
                    r, jb = bp // nb, bp % nb
                    t0 = r + d * 128 * jb
                    if t0 < SEQ - rows:
                        return None
                    start = t0 - (SEQ - rows)
                    return slice(start, start + d * 127 + 1, d)

                def inproj(w, wkey, c0, t, bk):
                    for c8 in range(8):
                        P.op("pe", lambda e, c8=c8: e.matmul(
                            PB(bk) if d < 16 else PB(bk).rearrange("p (r i) -> p r i", r=4),
                            w[:, c8, c0:c0 + 128], perm_tile(hT[:, c8, :], d, t),
                            start=(c8 == 0), stop=(c8 == 7)), reads=[wkey] + HTK, writes=[PK(bk)])

                wk, wkk = load_w_c((s, name, "k"), w_in_b, "w_in_b", 0, 1024, kcol, nk * 128)
                pend = [None]

                def flush():
                    if pend[0] is not None:
                        pend[0]()
                        pend[0] = None

                for t in range(4):
                    for kc in range(nk):
                        bk = rot("inbank", 2)
                        inproj(wk, wkk, kc * 128, t, bk)
                        ki = rot("ksum", 2)

                        def kstate(t=t, kc=kc, ki=ki):
                            for j in range(4):
                                sl = state_block(4 * t + j)
                                if sl is None:
                                    continue
                                P.op("pe", lambda e, j=j: e.transpose(
                                    PB(3)[:, 0:128], ksum[ki][:, j * 128:(j + 1) * 128], ident_f),
                                    reads=["ksum%d" % ki, "ident_f"], writes=[PK(3)])
                                P.op("dve", lambda e, j=j: e.tensor_copy(
                                    stgK[:, j, kc * 128:(kc + 1) * 128], PB(3)[:, 0:128]),
                                    reads=[PK(3)], writes=["stgK%d" % j])
                        p1, p2 = rope_parts(bk, perm_tile(cosT, d, t), perm_tile(sinT, d, t),
                                            KT[:, kc, t * 512:(t + 1) * 512], "KT",
                                            ks=ksum[ki], ks_key="ksum%d" % ki, shape4=(d == 16), after=kstate)
                        p1()
                        flush()
                        pend[0] = p2
                    if any(state_block(4 * t + j) is not None for j in range(4)):
                        flush()
                        for j in range(4):
                            sl = state_block(4 * t + j)
                            if sl is None:
                                continue
                            dma("act", out_t[s, sl, 0, :, :],
                                stgK[:, j, 0:ncv].rearrange("p (h e) -> p h e", h=H),
                                ["stgK%d" % j], ["out_" + name], final=True)
                flush()
                wv, wvk = load_w_c((s, name, "v"), w_in_b, "w_in_b", 0, 1024, vcol, ncv)
                for bp in range(16):
                    vb = 3 - (bp % 2)
                    for c8 in range(8):
                        P.op("pe", lambda e, c8=c8, bp=bp, vb=vb: e.matmul(
                            PB(vb)[:, 0:ncv], perm_block(hT[:, c8, :], d, bp), wv[:, c8, 0:ncv],
                            start=(c8 == 0), stop=(c8 == 7)), reads=[wvk] + HTK, writes=[PK(vb)])
                    P.op("act", lambda e, bp=bp, vb=vb: e.copy(
                        Vaug[:, bp, 0:H, 0:64], PB(vb)[:, 0:ncv].rearrange("p (h e) -> p h e", h=H)),
                        reads=[PK(vb)], writes=["Vaug_%d" % bp])
                    if deferred_final:
                        deferred_final.pop(0)()
                    sl = state_block(bp)
                    if sl is not None:
                        vi = rot("stgV", 2)
                        P.op("dve", lambda e, vi=vi, vb=vb: e.tensor_copy(stgV[vi][:, 0:ncv], PB(vb)[:, 0:ncv]),
                             reads=[PK(vb)], writes=["stgV%d" % vi])
                        dma("act", out_t[s, sl, 1, :, :],
                            stgV[vi][:, 0:ncv].rearrange("p (h e) -> p h e", h=H),
                            ["stgV%d" % vi], ["out_" + name], final=True)
                issue_bulk(1)
                wqs = []
                for q0 in range(0, nq * 128, 512):
                    wqs.append(load_w_c((s, name, "q", q0), w_in_b, "w_in_b", 0, 1024, qcol + q0, min(512, nq * 128 - q0)))
                ST_BANKS = [4, 5, 3]
                LA = 2

                def scores(Pr):
                    T0 = Pr[0]
                    sb, hp, pi = T0["sb"], T0["has_prev"], T0["pi"]
                    w = 256 * len(Pr)
                    P.op("pe", lambda e: e.matmul(
                        PB(sb)[:, 0:w], ident_bf, maskT[:, 0:w], start=True, stop=False),
                        reads=["ident_bf", "maskT"], writes=[PK(sb)])
                    for hi, T in enumerate(Pr):
                        hs, bp, j, qi, ql, kc = T["hs"], T["bp"], T["j"], T["qi"], T["ql"], T["kc"]
                        qk = "QT%d" % qi
                        off = 256 * hi
                        last = (hi == len(Pr) - 1)
                        if hp:
                            P.op("pe", lambda e, T=T, off=off, hs=hs, bp=bp, j=j, qi=qi, ql=ql, kc=kc: e.matmul(
                                PB(sb)[:, off:off + 128], KT[hs, kc, (bp - 1) * 128:bp * 128],
                                QT[qi][hs, ql, j * 128:(j + 1) * 128], start=False, stop=False),
                                reads=["KT", qk], writes=[PK(sb)])
                        P.op("pe", lambda e, T=T, off=off, hs=hs, bp=bp, j=j, qi=qi, ql=ql, kc=kc, last=last: e.matmul(
                            PB(sb)[:, off + 128:off + 256], KT[hs, kc, bp * 128:(bp + 1) * 128],
                            QT[qi][hs, ql, j * 128:(j + 1) * 128], start=False, stop=last),
                            reads=["KT", qk], writes=[PK(sb)])
                    if hp:
                        P.op("act", lambda e: e.activation(
                            out=PT[pi][:, 0:w], in_=PB(sb)[:, 0:w], func=AF.Exp, scale=0.125),
                            reads=[PK(sb)], writes=["PT%d" % pi])
                    else:
                        nh_ = len(Pr)
                        P.op("act", lambda e: e.activation(
                            out=PT[pi][:, 0:w].rearrange("p (h c) -> p h c", h=nh_)[:, :, 128:256],
                            in_=PB(sb)[:, 0:w].rearrange("p (h c) -> p h c", h=nh_)[:, :, 128:256],
                            func=AF.Exp, scale=0.125), reads=[PK(sb)], writes=["PT%d" % pi])

                def pv(Pr):
                    for hi, T in enumerate(Pr):
                        pob, slot, bp, vh, pi, hp = T["pob"], T["slot"], T["bp"], T["vh"], T["pi"], T["has_prev"]
                        off = 256 * hi
                        po = PB(pob)[:, slot * 128:(slot + 1) * 128]
                        first = True
                        if is_a and slot == 0:
                            hq0 = T["grp"][0][0] + 8 * T["grp"][0][1]
                            P.op("pe", lambda e, hq0=hq0, pob=pob: e.matmul(
                                PB(pob), sinkl[0:1, :], esrow[0:1, hq0 * 128:(hq0 + 4) * 128], start=True, stop=False),
                                reads=["sinkl", "esrow"], writes=[PK(pob)])
                        if is_a:
                            first = False
                        if hp:
                            P.op("pe", lambda e, po=po, bp=bp, vh=vh, pi=pi, off=off, first=first: e.matmul(
                                po, Vaug[:, bp - 1, vh, :], PT[pi][:, off:off + 128], start=first, stop=False),
                                reads=["Vaug_%d" % (bp - 1), "PT%d" % pi], writes=[PK(pob)])
                            first = False
                        P.op("pe", lambda e, po=po, bp=bp, vh=vh, pi=pi, off=off, first=first, T=T: e.matmul(
                            po, Vaug[:, bp, vh, :], PT[pi][:, off + 128:off + 256], start=first,
                            stop=(T["last"] or not is_a)),
                            reads=["Vaug_%d" % bp, "PT%d" % pi], writes=[PK(pob)])
                        if T["last"]:
                            evac(T)

                def evac(T):
                    pob, bp = T["pob"], T["bp"]
                    if is_a:
                        half, qc0 = T["grp"][0][1], T["grp"][0][0]
                        ri = rot("rden", 2)
                        P.op("act", lambda e: e.activation(out=rden[ri][0:64, :], in_=PB(pob)[64:128, :], func=AF.Ln),
                             reads=[PK(pob)], writes=["rden%d" % ri])
                        P.op("act", lambda e: e.activation(out=rden[ri][0:64, :], in_=rden[ri][0:64, :], func=AF.Exp,
                                                           scale=-1.0), reads=["rden%d" % ri], writes=["rden%d" % ri])
                        P.op("dve", lambda e: e.tensor_tensor(
                            oaT[half * 64:half * 64 + 64, qc0:qc0 + 4, bp * 128:(bp + 1) * 128],
                            PB(pob)[0:64, :].rearrange("p (c t) -> p c t", c=4),
                            rden[ri][0:64, :].rearrange("p (c t) -> p c t", c=4), op=ALU.mult),
                            reads=[PK(pob), "rden%d" % ri], writes=["oaT", "obacc"])
                    else:
                        r, jb = bp // nb, bp % nb
                        if d == 1:
                            ov = obacc[:, :, bp * 128:(bp + 1) * 128]
                        else:
                            ov = obacc.rearrange("p s (i r) -> p s r i", r=d)[:, :, r, jb * 128:(jb + 1) * 128]
                        pv4 = PB(pob).rearrange("p (c t) -> p c t", c=4)
                        if st["gi"] == 0:
                            P.op("act", lambda e: e.copy(ov, pv4), reads=[PK(pob)], writes=["obacc"])
                        else:
                            P.op("dve", lambda e: e.tensor_tensor(ov, pv4, ov, op=ALU.add),
                                 reads=[PK(pob), "obacc"], writes=["obacc"])

                units = [(t, qh) for t in range(4) for qh in range((nq + 3) // 4)]
                qis = {}

                def qproj(u):
                    t, qh = units[u]
                    qi = rot("QT", 2)
                    qis[u] = qi
                    nqc = min(4, nq - 4 * qh)
                    for ql in range(nqc):
                        qc = 4 * qh + ql
                        wq, wqk = wqs[qc // 4]
                        bk = rot("inbank", 2)
                        inproj(wq, wqk, (qc % 4) * 128, t, bk)
                        p1, p2 = rope_parts(bk, perm_tile(cosT, d, t), perm_tile(sinT, d, t),
                                            QT[qi][:, ql, :], "QT%d" % qi, shape4=(d == 16))
                        p1()
                        flush()
                        pend[0] = p2
                    flush()

                def attn(u):
                    t, qh = units[u]
                    qi = qis[u]
                    tasks = []
                    for j in range(4):
                        bp = 4 * t + j
                        has_prev = (bp % nb) != 0
                        if is_a:
                            groups = [[(qh * 4 + k, half) for k in range(4)] for half in range(2)]
                        else:
                            groups = [[(sl_ % 2, sl_ // 2) for sl_ in range(4)]]
                        for grp in groups:
                            pob = 6 + rot("pobank", 2)
                            for slot, (qc, half) in enumerate(grp):
                                tasks.append(dict(
                                    j=j, bp=bp, has_prev=has_prev, lo=0 if has_prev else 128, qc=qc, half=half,
                                    ql=qc - 4 * qh, kc=0 if is_a else qc, vh=half if is_a else 2 * qc + half,
                                    hs=slice(half * 64, half * 64 + 64), qi=qi, pob=pob, slot=slot, grp=grp,
                                    last=(slot == len(grp) - 1)))
                    pairs = [tasks[i:i + 2] for i in range(0, len(tasks), 2)]
                    for i, Pr in enumerate(pairs):
                        sb_ = ST_BANKS[rot("stbank", 3)]
                        pi_ = rot("PT", 4)
                        for T in Pr:
                            T["sb"] = sb_
                            T["pi"] = pi_
                        scores(Pr)
                        if i >= LA:
                            pv(pairs[i - LA])
                    for Pr in pairs[max(0, len(pairs) - LA):]:
                        pv(Pr)

                qproj(0)
                for u in range(len(units)):
                    if u + 1 < len(units):
                        qproj(u + 1)
                    attn(u)
                if st["gi"] == 2:
                    for tt in range(4):
                        for h in range(4):
                            def fin(tt=tt, h=h):
                                ri = rot("rden", 2)
                                P.op("act", lambda e: e.activation(
                                    out=rden[ri][0:64, :], in_=obacc[64:128, h, tt * 512:(tt + 1) * 512], func=AF.Ln),
                                    reads=["obacc"], writes=["rden%d" % ri])
                                P.op("act", lambda e: e.activation(
                                    out=rden[ri][0:64, :], in_=rden[ri][0:64, :], func=AF.Exp, scale=-1.0),
                                    reads=["rden%d" % ri], writes=["rden%d" % ri])
                                P.op("dve", lambda e: e.tensor_tensor(
                                    obT[(h // 2) * 64:(h // 2) * 64 + 64, h % 2, tt * 512:(tt + 1) * 512],
                                    obacc[0:64, h, tt * 512:(tt + 1) * 512], rden[ri][0:64, :], op=ALU.mult),
                                    reads=["obacc", "rden%d" % ri], writes=["obT"])
                            deferred_final.append(fin)

            for st in SETS:
                do_set(st)
            while deferred_final:
                deferred_final.pop(0)()
            if "3" in stages:
                for q in range(3):
                    prefetch_w((s, 0, "gate", q), w_in_b, "w_in_b", 0, 1024, GATE_COL + q * 512, 512)

        def stage3(s):
            def do_group(g):
                tok0 = g * 512
                mark("  group %d" % g)
                xt = xts[g % 2]
                XK = "xt%d_" % (g % 2)

                def load_and_norm1(gg):
                    xt_ = xts[gg % 2]
                    xk_ = "xt%d_" % (gg % 2)
                    for j in range(4):
                        dma("sp", xt_[:, j, :], xp[s, gg * 512 + j * 128:gg * 512 + (j + 1) * 128, :], [],
                            [xk_ + "%d" % j])
                    for j in range(4):
                        rms_norm(xt_[:, j, :], [xk_ + "%d" % j], g1b, "g1b", hn[j], ["hn%d" % j])
                if g == 0:
                    load_and_norm1(0)
                for j in range(4):
                    transpose_to(hn[j], "hn%d" % j, actT, "actT", j * 128, bank=4 + (j % 2))
                for q in range(4):
                    wg, wgk = load_w_c((s, g, "gate", q), w_in_b, "w_in_b", 0, 1024, GATE_COL + q * 512, 512)
                    for k in range(4):
                        gc = q * 4 + k
                        bk = rot("fbank", 4)
                        for c8 in range(8):
                            P.op("pe", lambda e, c8=c8, k=k, bk=bk, wg=wg: e.matmul(
                                PB(bk), wg[:, c8, k * 128:(k + 1) * 128], actT[:, c8, :],
                                start=(c8 == 0), stop=(c8 == 7)), reads=[wgk, "actT"], writes=[PK(bk)])
                        P.op("act", lambda e, gc=gc, bk=bk: e.activation(
                            out=gatesT[:, gc, :], in_=PB(bk), func=AF.Sigmoid),
                            reads=[PK(bk)], writes=["gatesT"])
                issue_bulk(2)
                wpa = [load_w(w_pa_b, "w_pa_b", 0, 1024, n * 512, 512) for n in range(2)]
                wpb = [load_w(w_pb_b, "w_pb_b", 0, 256, n * 512, 512) for n in range(2)]
                for oc in range(8):
                    bka = rot("fbank", 4)
                    w, wkey = wpa[oc // 4]
                    for c8 in range(8):
                        P.op("pe", lambda e, c8=c8, oc=oc, bka=bka, w=w: e.matmul(
                            PB(bka), w[:, c8, (oc % 4) * 128:(oc % 4 + 1) * 128], oaT[:, c8, tok0:tok0 + 512],
                            start=(c8 == 0), stop=(c8 == 7)), reads=[wkey, "oaT"], writes=[PK(bka)])
                    ia = rot("tmpA", 2)
                    P.op("dve", lambda e, ia=ia, bka=bka, oc=oc: e.tensor_tensor(
                        tmpA[ia], PB(bka), gatesT[:, oc, :], op=ALU.mult),
                        reads=[PK(bka), "gatesT"], writes=["tmpA%d" % ia])
                    bkb = rot("fbank", 4)
                    w, wkey = wpb[oc // 4]
                    for c2 in range(2):
                        P.op("pe", lambda e, c2=c2, oc=oc, bkb=bkb, w=w: e.matmul(
                            PB(bkb), w[:, c2, (oc % 4) * 128:(oc % 4 + 1) * 128], obT[:, c2, tok0:tok0 + 512],
                            start=(c2 == 0), stop=(c2 == 1)), reads=[wkey, "obT"], writes=[PK(bkb)])
                    ib = rot("tmpB", 2)
                    P.op("dve", lambda e, ib=ib, bkb=bkb, oc=oc: e.tensor_tensor(
                        tmpB[ib], PB(bkb), gatesT[:, 8 + oc, :], op=ALU.mult),
                        reads=[PK(bkb), "gatesT"], writes=["tmpB%d" % ib])
                    P.op("pool", lambda e, ia=ia, ib=ib, oc=oc: e.tensor_tensor(
                        mT[:, oc, :], tmpA[ia], tmpB[ib], op=ALU.add),
                        reads=["tmpA%d" % ia, "tmpB%d" % ib], writes=["mT"])
                wo = [load_w(w_o_b, "w_o_b", 0, 1024, n * 512, 512) for n in range(2)]
                for j in range(4):
                    for n in range(2):
                        bk = rot("fbank", 4)
                        w, wkey = wo[n]
                        for c8 in range(8):
                            P.op("pe", lambda e, c8=c8, j=j, bk=bk, w=w: e.matmul(
                                PB(bk), mT[:, c8, j * 128:(j + 1) * 128], w[:, c8, :],
                                start=(c8 == 0), stop=(c8 == 7)), reads=[wkey, "mT"], writes=[PK(bk)])
                        P.op("dve", lambda e, j=j, n=n, bk=bk: e.tensor_tensor(
                            xt[:, j, n * 512:(n + 1) * 512], PB(bk), xt[:, j, n * 512:(n + 1) * 512], op=ALU.add),
                            reads=[PK(bk), XK + "%d" % j], writes=[XK + "%d" % j])
                for j in range(4):
                    rms_norm(xt[:, j, :], [XK + "%d" % j], g2b, "g2b", hn[j], ["hn%d" % j])
                for j in range(4):
                    transpose_to(hn[j], "hn%d" % j, actT, "actT", j * 128, bank=4 + (j % 2))
                if g + 1 < 4:
                    load_and_norm1(g + 1)
                for q in range(6):
                    ncol = 512 if q < 5 else 256
                    wgt, wgtk = load_w(w_gu_b, "w_gu_b", 0, 1024, q * 512, ncol)
                    wup, wupk = load_w(w_gu_b, "w_gu_b", 0, 1024, D_FF + q * 512, ncol)
                    for k in range(ncol // 128):
                        fc = q * 4 + k
                        bg = 4 + rot("gbank", 2)
                        bu = 6 + rot("ubank", 2)
                        for c8 in range(8):
                            P.op("pe", lambda e, c8=c8, k=k, bg=bg, wgt=wgt: e.matmul(
                                PB(bg), wgt[:, c8, k * 128:(k + 1) * 128], actT[:, c8, :],
                                start=(c8 == 0), stop=(c8 == 7)), reads=[wgtk, "actT"], writes=[PK(bg)])
                        for c8 in range(8):
                            P.op("pe", lambda e, c8=c8, k=k, bu=bu, wup=wup: e.matmul(
                                PB(bu), wup[:, c8, k * 128:(k + 1) * 128], actT[:, c8, :],
                                start=(c8 == 0), stop=(c8 == 7)), reads=[wupk, "actT"], writes=[PK(bu)])
                        si = rot("sgt", 2)
                        P.op("act", lambda e, si=si, bg=bg: e.activation(out=sgt[si], in_=PB(bg), func=AF.Silu),
                             reads=[PK(bg)], writes=["sgt%d" % si])
                        P.op("dve", lambda e, si=si, bu=bu, fc=fc: e.tensor_tensor(
                            aT[:, fc, :], PB(bu), sgt[si], op=ALU.mult),
                            reads=[PK(bu), "sgt%d" % si], writes=["aT"])
                for n in range(2):
                    for wi, (f0, nf) in enumerate([(0, 8), (8, 8), (16, 6)]):
                        w, wkey = load_w(w_dn_b, "w_dn_b", f0 * 128, nf * 128, n * 512, 512)
                        for j in range(4):
                            for f in range(nf):
                                fc = f0 + f
                                P.op("pe", lambda e, j=j, f=f, fc=fc, w=w: e.matmul(
                                    PB(j), aT[:, fc, j * 128:(j + 1) * 128], w[:, f, :],
                                    start=(fc == 0), stop=(fc == 21)), reads=[wkey, "aT"], writes=[PK(j)])
                    for j in range(4):
                        P.op("dve", lambda e, j=j, n=n: e.tensor_tensor(
                            xt[:, j, n * 512:(n + 1) * 512], PB(j), xt[:, j, n * 512:(n + 1) * 512], op=ALU.add),
                            reads=[PK(j), XK + "%d" % j], writes=[XK + "%d" % j])
                for j in range(4):
                    yi = rot("yst", 2)
                    rms_norm(xt[:, j, :], [XK + "%d" % j], gfb, "gfb", yst[yi], ["yst%d" % yi])
                    dma("pool", yp[s, tok0 + j * 128:tok0 + (j + 1) * 128, :], yst[yi],
                        ["yst%d" % yi], ["yp"], final=True)

            for g in range(4):
                do_group(g)

        bulk_list = []

        def sample_copies():
            bulk_list.append((sa_o[:, 0:127], ca_d[:, 1:128], "sa_copy"))
            bulk_list.append((sb_o[0][:, 0:127], cb_d[0][:, 1:128], "sb1_copy"))
            for b in range(0, NS, 4):
                bulk_list.append((sb_o[1][b:b + 4, 0:511], cb_d[1][b:b + 4, 1:512], "sb2_copy%d" % b))
            for b in range(NS):
                bulk_list.append((sb_o[2][b, 0:2047], cb_d[2][b, 1:2048], "sb3_copy%d" % b))

        def issue_bulk(n):
            for _ in range(n):
                if bulk_list:
                    o_, i_, k_ = bulk_list.pop(0)
                    dma("act", o_, i_, [], [k_], final=True, bulk=True)

        def tm_matmul(lhs3, lkey, nk, w_b, wkey, r0, c0, ncols, consume):
            for n0 in range(0, ncols, 512):
                n = min(512, ncols - n0)
                pieces = [(k0, min(8, nk - k0)) for k0 in range(0, nk, 8)]
                bk = rot("sbank", 4)
                for (k0, kn) in pieces:
                    w, wk_ = load_w(w_b, wkey, r0 + k0 * 128, kn * 128, c0 + n0, n)
                    for k in range(kn):
                        P.op("pe", lambda e, k=k, k0=k0, w=w, bk=bk, n=n: e.matmul(
                            PB(bk)[0:NS, 0:n], lhs3[:, k0 + k, 0:NS], w[:, k, 0:n],
                            start=(k0 + k == 0), stop=(k0 + k == nk - 1)),
                            reads=[lkey, wk_], writes=[PK(bk)])
                consume(bk, n0, n)

        def sample_phase():
            N = NS
            dma("sp", S_xs[0:N, :], xs_d, [], ["S_xs"])
            dma("sp", S_sel[0:16], c_sel.rearrange("p (b m) -> p b m", b=16), [], ["S_sel"])
            dma("sp", S_eye[0:16, :], c_eye16, [], ["S_eye"])
            dma("sp", S_cs[0:16, :], c_cs, [], ["S_cs"])
            for q in range(2):
                for i in range(4):
                    P.op("dve", lambda e, q=q, i=i: e.memset(S_va[q][i][:, :, 64:128], 1.0),
                         writes=["S_va%d_%d" % (q, i)])
            P.op("dve", lambda e: e.memset(S_vnew[0:N, :, 64:128], 1.0), writes=["S_vnew"])
            rms_norm(S_xs[0:N, :], ["S_xs"], g1b, "g1b", S_hn[0:N, :], ["S_hn"], npart=N)
            transpose_to(S_hn[0:N, :], "S_hn", S_hT, "S_hT", 0, npart=N)
            def ev_z(bk, c0, n):
                P.op("act", lambda e: e.copy(S_z[0:N, c0:c0 + n], PB(bk)[0:N, 0:n]), reads=[PK(bk)], writes=["S_z"])
            tm_matmul(S_hT, "S_hT", 8, w_in_b, "w_in_b", 0, 0, N_IN, ev_z)
            if SU_ == "inproj":
                return
            P.op("act", lambda e: e.activation(out=S_g[0:N, :], in_=S_z[0:N, GATE_COL:GATE_COL + 2048], func=AF.Sigmoid),
                 reads=["S_z"], writes=["S_g"])
            cosb = S_cs[0:N, 0:64]
            sinb = S_cs[0:N, 64:128]
            for (c0, nh) in [(0, 18), (1280, 8), (1280 + 768, 8), (1280 + 1536, 8)]:
                zv = S_z[0:N, c0:c0 + nh * 64].rearrange("p (h e) -> p h e", h=nh)
                a1 = S_t1[0:N, 0:nh * 64].rearrange("p (h e) -> p h e", h=nh)
                a2 = S_t2[0:N, 0:nh * 64].rearrange("p (h e) -> p h e", h=nh)
                P.op("dve", lambda e, zv=zv, a1=a1, nh=nh: e.tensor_tensor(
                    a1, zv, cosb[:, None, :].broadcast_to([N, nh, 64]), op=ALU.mult),
                    reads=["S_z", "S_cs"], writes=["S_t1"])
                P.op("dve", lambda e, zv=zv, a2=a2, nh=nh: e.tensor_tensor(
                    a2[:, :, 0:32], zv[:, :, 32:64], sinb[:, None, 0:32].broadcast_to([N, nh, 32]), op=ALU.mult),
                    reads=["S_z", "S_cs"], writes=["S_t2"])
                P.op("dve", lambda e, zv=zv, a2=a2, nh=nh: e.tensor_tensor(
                    a2[:, :, 32:64], zv[:, :, 0:32], sinb[:, None, 32:64].broadcast_to([N, nh, 32]), op=ALU.mult),
                    reads=["S_z", "S_cs"], writes=["S_t2"])
                P.op("dve", lambda e, zv=zv, a1=a1, a2=a2: e.tensor_tensor(zv, a1, a2, op=ALU.add),
                     reads=["S_t1", "S_t2"], writes=["S_z"])
            P.op("act", lambda e: e.copy(S_zq[0:N, :], S_z[0:N, 0:3584]), reads=["S_z"], writes=["S_zq"])
            sets_s = [dict(kc=1024, vc=1152, H=2, W=128, out=sa_o, hv0=0)]
            for g in range(3):
                base = 1280 + 768 * g
                sets_s.append(dict(kc=base + 256, vc=base + 512, H=4, W=[128, 512, 2048][g], out=sb_o[g], hv0=2 + 4 * g))
            for si, ss_ in enumerate(sets_s):
                H = ss_["H"]
                nv = S_new[si][0:N, :].rearrange("p (k f) -> p k f", k=2)
                P.op("act", lambda e, nv=nv, ss_=ss_, H=H: e.copy(nv[:, 0, :], S_z[0:N, ss_["kc"]:ss_["kc"] + H * 64]),
                     reads=["S_z"], writes=["S_new%d" % si])
                P.op("act", lambda e, nv=nv, ss_=ss_, H=H: e.copy(nv[:, 1, :], S_z[0:N, ss_["vc"]:ss_["vc"] + H * 64]),
                     reads=["S_z"], writes=["S_new%d" % si])
                dma("pool", ss_["out"][:, ss_["W"] - 1].rearrange("b k h e -> b (k h e)"), S_new[si][0:N, :],
                    ["S_new%d" % si], ["snew_out%d" % si], final=True)
                P.op("act", lambda e, ss_=ss_, H=H: e.copy(
                    S_vnew[0:N, ss_["hv0"]:ss_["hv0"] + H, 0:64],
                    S_z[0:N, ss_["vc"]:ss_["vc"] + H * 64].rearrange("p (h e) -> p h e", h=H)),
                    reads=["S_z"], writes=["S_vnew"])
            qa = S_z[0:N, 0:1024].rearrange("p (c f e) -> p c f e", c=8, f=2)
            ka = S_z[0:N, 1024:1152].rearrange("p (f e) -> p f e", f=2)
            pnA = S_pn[0:N, 0:1024].rearrange("p (c f e) -> p c f e", c=8, f=2)
            P.op("dve", lambda e: e.tensor_tensor(pnA, qa, ka[:, None, :, :].broadcast_to([N, 8, 2, 64]), op=ALU.mult),
                 reads=["S_z"], writes=["S_pn"])
            for g in range(3):
                base = 1280 + 768 * g
                P.op("dve", lambda e, g=g, base=base: e.tensor_tensor(
                    S_pn[0:N, 1024 + 256 * g:1024 + 256 * (g + 1)], S_z[0:N, base:base + 256],
                    S_z[0:N, base + 256:base + 512], op=ALU.mult), reads=["S_z"], writes=["S_pn"])
            P.op("dve", lambda e: e.tensor_reduce(S_pnew[0:N, :], S_pn[0:N, :].rearrange("p (h e) -> p h e", h=28),
                                                  axis=AX.X, op=ALU.add), reads=["S_pn"], writes=["S_pnew"])
            P.op("act", lambda e: e.activation(out=S_pnew[0:N, :], in_=S_pnew[0:N, :], func=AF.Exp, scale=0.125),
                 reads=["S_pnew"], writes=["S_pnew"])
            P.op("dve", lambda e: e.tensor_tensor(
                S_pd[0:N], S_pnew[0:N, :][:, None, :].broadcast_to([N, 16, 28]),
                S_eye[0:N, :][:, :, None].broadcast_to([N, 16, 28]), op=ALU.mult),
                reads=["S_pnew", "S_eye"], writes=["S_pd"])
            if SU_ == "pre":
                return
            OA, OB = 5, 6
            es_v = esrow[0:1, :].rearrange("p (h n) -> p h n", h=16)
            for b in range(N):
                q2 = b % 2
                for (bk, c0, n, o0) in [(0, 0, 512, 0), (1, 512, 512, 0), (2, 1280, 256, 0), (2, 1280 + 768, 256, 256),
                                        (3, 1280 + 1536, 256, 0)]:
                    P.op("pe", lambda e, b=b, bk=bk, c0=c0, n=n, o0=o0: e.matmul(
                        PB(bk)[:, o0:o0 + n], S_sel[0:16, b, :], S_zq[0:16, c0:c0 + n], start=True, stop=True),
                        reads=["S_sel", "S_zq"], writes=[PK(bk)])
                pts = []
                for si in range(4):
                    H = 2 if si == 0 else 4
                    src = ca_d if si == 0 else cb_d[si - 1]
                    dd = [1, 1, 4, 16][si]
                    kvt = S_kv[q2][si]
                    kvk = "S_kv%d_%d" % (q2, si)
                    dma("sp", kvt, src[b, 0:128 * dd:dd].rearrange("j k h e -> j (k h e)"), [], [kvk])
                    P.op("act", lambda e, kvt=kvt, H=H, q2=q2, si=si: e.copy(
                        S_va[q2][si][:, 0:H, 0:64], kvt[:, H * 64:2 * H * 64].rearrange("p (h e) -> p h e", h=H)),
                        reads=[kvk], writes=["S_va%d_%d" % (q2, si)])
                    if si == 0:
                        kview = kvt[:, 0:128].rearrange("p (f e) -> p f e", f=2)[:, None, :, :].broadcast_to([128, 8, 2, 64])
                        qv = PB(0).rearrange("p (c f e) -> p c f e", c=4, f=2)
                        pv_ = S_prod[:, 0:1024].rearrange("p (c f e) -> p c f e", c=8, f=2)
                        P.op("dve", lambda e, kview=kview, pv_=pv_: e.tensor_tensor(
                            pv_[:, 0:4], PB(0).rearrange("p (c f e) -> p c f e", c=4, f=2), kview[:, 0:4], op=ALU.mult),
                            reads=[PK(0), kvk], writes=["S_prod"])
                        P.op("dve", lambda e, kview=kview, pv_=pv_: e.tensor_tensor(
                            pv_[:, 4:8], PB(1).rearrange("p (c f e) -> p c f e", c=4, f=2), kview[:, 4:8], op=ALU.mult),
                            reads=[PK(1), kvk], writes=["S_prod"])
                        nh = 16
                    else:
                        bk, o0 = [(2, 0), (2, 256), (3, 0)][si - 1]
                        P.op("dve", lambda e, kvt=kvt, bk=bk, o0=o0: e.tensor_tensor(
                            S_prod[:, 0:256], PB(bk)[:, o0:o0 + 256], kvt[:, 0:256], op=ALU.mult),
                            reads=[PK(bk), kvk], writes=["S_prod"])
                        nh = 4
                    P.op("dve", lambda e, nh=nh: e.tensor_reduce(
                        S_st[:, 0:nh], S_prod[:, 0:nh * 64].rearrange("p (h e) -> p h e", h=nh), axis=AX.X, op=ALU.add),
                        reads=["S_prod"], writes=["S_st"])
                    po0 = 0 if si == 0 else 16 + 4 * (si - 1)
                    P.op("act", lambda e, nh=nh, q2=q2, po0=po0: e.activation(
                        out=S_pt[q2][:, po0:po0 + nh], in_=S_st[:, 0:nh], func=AF.Exp, scale=0.125),
                        reads=["S_st"], writes=["S_pt%d" % q2])
                ptk = "S_pt%d" % q2
                for half in range(2):
                    oa_ap = PB(OA)[:, b * 16 + half * 8:b * 16 + half * 8 + 8]
                    P.op("pe", lambda e, oa_ap=oa_ap, q2=q2, half=half: e.matmul(
                        oa_ap, S_va[q2][0][:, half, :], S_pt[q2][:, half:16:2], start=True, stop=False),
                        reads=["S_va%d_0" % q2, ptk], writes=[PK(OA)])
                    P.op("pe", lambda e, oa_ap=oa_ap, b=b, half=half: e.matmul(
                        oa_ap, S_vnew[0:16, half, :], S_pd[0:16, b, half:16:2], start=False, stop=False),
                        reads=["S_vnew", "S_pd"], writes=[PK(OA)])
                    P.op("pe", lambda e, oa_ap=oa_ap, half=half: e.matmul(
                        oa_ap, sinkl[0:1, :], es_v[:, half * 8:half * 8 + 8, 0], start=False, stop=True),
                        reads=["sinkl", "esrow"], writes=[PK(OA)])
                for h in range(4):
                    ob_ap = PB(OB)[:, b * 4 + h:b * 4 + h + 1]
                    for g in range(3):
                        P.op("pe", lambda e, ob_ap=ob_ap, q2=q2, h=h, g=g: e.matmul(
                            ob_ap, S_va[q2][1 + g][:, h, :], S_pt[q2][:, 16 + 4 * g + h:16 + 4 * g + h + 1],
                            start=(g == 0), stop=False), reads=["S_va%d_%d" % (q2, 1 + g), ptk], writes=[PK(OB)])
                        P.op("pe", lambda e, ob_ap=ob_ap, b=b, h=h, g=g: e.matmul(
                            ob_ap, S_vnew[0:16, 2 + 4 * g + h, :], S_pd[0:16, b, 16 + 4 * g + h:16 + 4 * g + h + 1],
                            start=False, stop=(g == 2)), reads=["S_vnew", "S_pd"], writes=[PK(OB)])
            if SU_ == "attn":
                return
            P.op("dve", lambda e: e.reciprocal(S_rden[0:64, 0:256], PB(OA)[64:128, 0:256]), reads=[PK(OA)], writes=["S_rden"])
            for half in range(2):
                P.op("dve", lambda e, half=half: e.tensor_tensor(
                    S_oaT[half * 64:half * 64 + 64, :, 0:N].rearrange("p c b -> p b c"),
                    PB(OA)[0:64, 0:256].rearrange("p (b f c) -> p b f c", b=16, f=2)[:, 0:N, half, :],
                    S_rden[0:64, 0:256].rearrange("p (b f c) -> p b f c", b=16, f=2)[:, 0:N, half, :], op=ALU.mult),
                    reads=[PK(OA), "S_rden"], writes=["S_oaT"])
            P.op("dve", lambda e: e.reciprocal(S_rden[0:64, 0:64], PB(OB)[64:128, 0:64]), reads=[PK(OB)], writes=["S_rden"])
            for half in range(2):
                P.op("dve", lambda e, half=half: e.tensor_tensor(
                    S_obT[half * 64:half * 64 + 64, :, 0:N].rearrange("p c b -> p b c"),
                    PB(OB)[0:64, 0:64].rearrange("p (b c f) -> p b c f", b=16, c=2)[:, 0:N, :, half],
                    S_rden[0:64, 0:64].rearrange("p (b c f) -> p b c f", b=16, c=2)[:, 0:N, :, half], op=ALU.mult),
                    reads=[PK(OB), "S_rden"], writes=["S_obT"])
            if SU_ == "norm":
                return
            def ev_pa(bk, c0, n):
                P.op("dve", lambda e: e.tensor_tensor(S_pa[0:N, c0:c0 + n], PB(bk)[0:N, 0:n], S_g[0:N, c0:c0 + n], op=ALU.mult),
                     reads=[PK(bk), "S_g"], writes=["S_pa"])
            tm_matmul(S_oaT, "S_oaT", 8, w_pa_b, "w_pa_b", 0, 0, 1024, ev_pa)

            def ev_pb(bk, c0, n):
                P.op("dve", lambda e: e.tensor_tensor(S_pb[0:N, c0:c0 + n], PB(bk)[0:N, 0:n], S_g[0:N, 1024 + c0:1024 + c0 + n], op=ALU.mult),
                     reads=[PK(bk), "S_g"], writes=["S_pb"])
            tm_matmul(S_obT, "S_obT", 2, w_pb_b, "w_pb_b", 0, 0, 1024, ev_pb)
            P.op("dve", lambda e: e.tensor_tensor(S_mbf[0:N, :], S_pa[0:N, :], S_pb[0:N, :], op=ALU.add),
                 reads=["S_pa", "S_pb"], writes=["S_mbf"])
            transpose_to(S_mbf[0:N, :], "S_mbf", S_mT, "S_mT", 0, npart=N)

            if SU_ == "proj":
                return

            def ev_x1(bk, c0, n):
                P.op("dve", lambda e: e.tensor_tensor(S_xs[0:N, c0:c0 + n], PB(bk)[0:N, 0:n], S_xs[0:N, c0:c0 + n], op=ALU.add),
                     reads=[PK(bk), "S_xs"], writes=["S_xs"])
            tm_matmul(S_mT, "S_mT", 8, w_o_b, "w_o_b", 0, 0, 1024, ev_x1)
            if SU_ == "wo":
                return
            rms_norm(S_xs[0:N, :], ["S_xs"], g2b, "g2b", S_hn[0:N, :], ["S_hn"], npart=N)
            transpose_to(S_hn[0:N, :], "S_hn", S_hT, "S_hT", 0, npart=N)

            def ev_gu(bk, c0, n):
                P.op("act", lambda e: e.copy(S_z[0:N, c0:c0 + n], PB(bk)[0:N, 0:n]), reads=[PK(bk)], writes=["S_z"])
            tm_matmul(S_hT, "S_hT", 8, w_gu_b, "w_gu_b", 0, 0, 2 * D_FF, ev_gu)
            if SU_ == "gu":
                return
            P.op("act", lambda e: e.activation(out=S_z[0:N, 0:D_FF], in_=S_z[0:N, 0:D_FF], func=AF.Silu),
                 reads=["S_z"], writes=["S_z"])
            P.op("dve", lambda e: e.tensor_tensor(S_abf[0:N, :], S_z[0:N, 0:D_FF], S_z[0:N, D_FF:2 * D_FF], op=ALU.mult),
                 reads=["S_z"], writes=["S_abf"])
            pvb = PB(4).bitcast(BF16).rearrange("p (c t) -> p c t", c=32)
            for fc in range(22):
                P.op("pe", lambda e, fc=fc: e.transpose(pvb[:, fc, 0:N], S_abf[0:N, fc * 128:(fc + 1) * 128],
                                                        ident_bf[0:N, 0:N]),
                     reads=["S_abf", "ident_bf"], writes=[PK(4)])
            P.op("act", lambda e: e.copy(S_aT[:, :, 0:N], pvb[:, 0:22, 0:N]), reads=[PK(4)], writes=["S_aT"])

            if SU_ == "a":
                return

            def ev_dn(bk, c0, n):
                P.op("dve", lambda e: e.tensor_tensor(S_xs[0:N, c0:c0 + n], PB(bk)[0:N, 0:n], S_xs[0:N, c0:c0 + n], op=ALU.add),
                     reads=[PK(bk), "S_xs"], writes=["S_xs"])
            tm_matmul(S_aT, "S_aT", 22, w_dn_b, "w_dn_b", 0, 0, 1024, ev_dn)
            rms_norm(S_xs[0:N, :], ["S_xs"], gfb, "gfb", S_y[0:N, :], ["S_y"], npart=N)
            dma("pool", ys_o, S_y[0:N, :], ["S_y"], ["ys_out"], final=True)

        if sets is not None:
            SETS[:] = [st for st in SETS if st["name"] in sets]
        import os
        SP_ = os.environ.get("SAMPLE_PARTS", "cp")
        SU_ = os.environ.get("S_UPTO", "")
        if do_sample and "c" in SP_:
            sample_copies()
        def mark(name):
            P.marks.append((name, sum(1 for o in P.ops if o["eng"] == "pe")))
        P.marks = []
        if do_prompt:
            for s in range(n_seq):
                if "1" in stages:
                    mark("s%d stage1" % s)
                    stage1(s)
                    P.barrier()
                if s == 0:
                    cast_rest()
                if "2" in stages:
                    mark("s%d stage2" % s)
                    stage2(s)
                    P.barrier()
                if "3" in stages:
                    mark("s%d stage3" % s)
                    stage3(s)
                    P.barrier()
        if "w_pa_b" not in wkeys:
            cast_rest()
        mark("sample")
        issue_bulk(len(bulk_list))
        if do_sample and "p" in SP_:
            sample_phase()
            P.barrier()
        if dbg is not None:
            dbg_o = dram("dbg", [128, 8 * SEQ], BF16, OUT)
            src = {"hT": (hT, "hT"), "oaT": (oaT, "oaT")}[dbg]
            dma("sp", dbg_o, src[0].rearrange("p c t -> p (c t)"), [src[1]], ["dbg"], final=True)
        P.emit()
    return nc, P


def host_consts():
    bf = ml_dtypes.bfloat16
    ident = np.eye(128, dtype=np.float32)
    k = np.arange(128)[:, None]
    q = np.arange(128)[None, :]
    mask = np.zeros((128, 512), np.float32)
    mask[:, 0:128] = np.where(k >= q, 0.0, NEG)
    mask[:, 128:256] = np.where(k <= q, 0.0, NEG)
    mask[:, 256:512] = mask[:, 0:256]
    rot = np.zeros((128, 128), np.float32)
    for m in range(128):
        if (m % 64) < 32:
            rot[m + 32, m] = -1.0
        else:
            rot[m - 32, m] = 1.0
    half = 32
    inv = (1.0 / (10000.0 ** (np.arange(half, dtype=np.float32) / half))).astype(np.float32)
    pos = np.arange(SEQ, dtype=np.float32)
    ang = pos[None, :] * inv[(np.arange(128) % 64) % 32][:, None]
    ang = ang.astype(np.float32)
    sinkl = np.zeros((1, 128), np.float32)
    sinkl[0, 64:] = 1.0
    sel = np.zeros((16, 16, 128), np.float32)
    for b in range(16):
        sel[b, b, :] = 1.0
    ang_s = (np.float32(PAST_LEN) * inv).astype(np.float32)
    ang_s = np.concatenate([ang_s, ang_s])
    cs = np.zeros((16, 128), np.float32)
    cs[:, 0:64] = np.cos(ang_s)[None, :]
    ssin = np.sin(ang_s).astype(np.float32)
    ssin[0:32] = -ssin[0:32]
    cs[:, 64:128] = ssin[None, :]
    return dict(c_ident_bf=ident.astype(bf), c_ident_f=ident, c_mask=mask.astype(bf),
                c_rot=rot.astype(bf), c_cos=np.cos(ang).astype(np.float32),
                c_sin=np.sin(ang).astype(np.float32), c_sinkl=sinkl.astype(bf),
                c_sel=sel.reshape(16, 16 * 128).astype(bf), c_eye16=np.eye(16, dtype=np.float32), c_cs=cs)


def host_weights(w_in, w_pa):
    w_in = np.asarray(w_in)[0]
    w_pa = np.asarray(w_pa)[0]
    qa = w_in[:, 0:1024].reshape(1024, 16, 64)
    perm_heads = []
    for c in range(8):
        perm_heads += [c, 8 + c]
    qa_p = qa[:, perm_heads, :].reshape(1024, 1024)
    ka = w_in[:, 1024:1152]
    va = w_in[:, 1152:1280]
    o3 = 1280
    qb = w_in[:, o3:o3 + 768].reshape(1024, 3, 256)
    kb = w_in[:, o3 + 768:o3 + 1536].reshape(1024, 3, 256)
    vb = w_in[:, o3 + 1536:o3 + 2304].reshape(1024, 3, 256)
    gates = w_in[:, o3 + 2304:]
    cols = [qa_p, ka, va]
    for g in range(3):
        cols += [qb[:, g], kb[:, g], vb[:, g]]
    cols.append(gates)
    w_in_dev = np.ascontiguousarray(np.concatenate(cols, axis=1), dtype=np.float32)
    assert w_in_dev.shape == (1024, N_IN)
    w_pa_dev = np.ascontiguousarray(w_pa.reshape(16, 64, 1024)[perm_heads].reshape(1024, 1024))
    return w_in_dev, w_pa_dev


_CACHE = {}


def kernel(x_prompt, x_sample, cache_a_kv, cache_b1_kv, cache_b2_kv, cache_b3_kv, norm1_g, w_in, sinks,
           w_pa, w_pb, w_o, norm2_g, w_gu, w_down, final_norm_g):
    n_seq = 2
    if "nc" not in _CACHE:
        _CACHE["nc"] = build_program(n_seq=n_seq)
    nc, _ = _CACHE["nc"]
    consts = host_consts()
    w_in_dev, w_pa_dev = host_weights(w_in, w_pa)
    shared = dict(consts)
    shared.update(w_in=w_in_dev, w_pa=w_pa_dev, w_pb=np.ascontiguousarray(np.asarray(w_pb)[0]),
                  w_o=np.ascontiguousarray(np.asarray(w_o)[0]), w_gu=np.ascontiguousarray(np.asarray(w_gu)[0]),
                  w_dn=np.ascontiguousarray(np.asarray(w_down)[0]),
                  g1=np.asarray(norm1_g).reshape(1, 1024), g2=np.asarray(norm2_g).reshape(1, 1024),
                  gf=np.asarray(final_norm_g).reshape(1, 1024), sinks=np.asarray(sinks).reshape(1, 16))
    x_prompt = np.asarray(x_prompt)
    x_sample = np.asarray(x_sample).reshape(128, D_MODEL)
    ca = np.asarray(cache_a_kv)[0]
    cbs = [np.asarray(cache_b1_kv)[0], np.asarray(cache_b2_kv)[0], np.asarray(cache_b3_kv)[0]]
    in_maps = []
    for c in range(N_CORES):
        m = dict(shared)
        m["xp"] = np.ascontiguousarray(x_prompt[c * n_seq:(c + 1) * n_seq])
        sl = slice(c * 16, (c + 1) * 16)
        m["xs"] = np.ascontiguousarray(x_sample[sl])
        m["ca"] = np.ascontiguousarray(ca[sl])
        for i in range(3):
            m["cb%d" % (i + 1)] = np.ascontiguousarray(cbs[i][sl])
        in_maps.append(m)
    res = run_bass_kernel_spmd(nc, in_maps, core_ids=list(range(N_CORES)))
    R = res.results
    y_prompt = np.concatenate([r["yp"] for r in R], axis=0)
    pa = np.concatenate([r["pa"] for r in R], axis=0)[None]
    pb1 = np.concatenate([r["pb1"] for r in R], axis=0)[None]
    pb2 = np.concatenate([r["pb2"] for r in R], axis=0)[None]
    pb3 = np.concatenate([r["pb3"] for r in R], axis=0)[None]
    y_sample = np.concatenate([r["ys"] for r in R], axis=0).reshape(128, 1, D_MODEL)
    sa = np.concatenate([r["sa"] for r in R], axis=0)[None]
    sb1 = np.concatenate([r["sb1"] for r in R], axis=0)[None]
    sb2 = np.concatenate([r["sb2"] for r in R], axis=0)[None]
    sb3 = np.concatenate([r["sb3"] for r in R], axis=0)[None]
    return (y_prompt, y_sample, pa, pb1, pb2, pb3, sa, sb1, sb2, sb3)
```

```python
import contextlib
import numpy as np
import ml_dtypes
import concourse.bass as bass
import concourse.mybir as mybir
from concourse.bass_utils import run_bass_kernel_spmd

F32 = mybir.dt.float32
BF16 = mybir.dt.bfloat16
I32 = mybir.dt.int32
AF = mybir.ActivationFunctionType
ALU = mybir.AluOpType
AX = mybir.AxisListType

D_MODEL = 1024
SEQ = 2048
D_FF = 2816
N_IN = 5632
NEG = -30000.0
EPS = 1e-6
PAST_LEN = 16384
N_CORES = 8

ENGS = ("pe", "act", "dve", "pool", "sp")
N_DMA_SEMS = 60
N_HW_SEMS = 24
N_SW_SEMS = 28


class Prog:
    def __init__(self, nc):
        self.nc = nc
        self.ops = []

    def op(self, eng, fn, reads=(), writes=(), dma=False, final=False, bulk=False, d2d=False):
        reads = tuple(reads)
        writes = tuple(writes) + tuple(k for k in reads if k.startswith("psum") and k not in writes)
        self.ops.append(dict(eng=eng, fn=fn, reads=reads, writes=writes, dma=dma,
                             final=final, barrier=False, bulk=bulk, d2d=d2d))

    def barrier(self):
        self.ops.append(dict(eng=None, fn=None, reads=(), writes=(), dma=False, final=False,
                             barrier=True, bulk=False, d2d=False))

    def emit(self):
        nc = self.nc
        ops = self.ops
        n = len(ops)
        last_w = {}
        readers = {}
        deps = [None] * n
        for i, o in enumerate(ops):
            d = set()
            if o["barrier"]:
                deps[i] = d
                continue
            for k in o["reads"]:
                if k in last_w:
                    d.add(last_w[k])
            for k in o["writes"]:
                if k in last_w:
                    d.add(last_w[k])
                for r in readers.get(k, ()):
                    d.add(r)
            d.discard(i)
            for k in o["writes"]:
                last_w[k] = i
                readers[k] = []
            for k in o["reads"]:
                if k not in o["writes"]:
                    readers.setdefault(k, []).append(i)
            if o["eng"] == "pe":
                d = {j for j in d if not (ops[j]["eng"] == "pe" and not ops[j]["dma"])}
            deps[i] = d
        slot_prev = [None] * N_DMA_SEMS
        slot_val = [0] * N_DMA_SEMS
        nd_hw = 0
        nd_sw = 0
        nd_bulk = 0
        for i, o in enumerate(ops):
            if o["dma"]:
                if o["bulk"]:
                    s = N_HW_SEMS + N_SW_SEMS + nd_bulk % (N_DMA_SEMS - N_HW_SEMS - N_SW_SEMS)
                    nd_bulk += 1
                elif o["eng"] == "pool":
                    s = N_HW_SEMS + nd_sw % N_SW_SEMS
                    nd_sw += 1
                else:
                    s = nd_hw % N_HW_SEMS
                    nd_hw += 1
                o["slot"] = s
                if slot_prev[s] is not None:
                    deps[i].add(slot_prev[s])
                slot_prev[s] = i
                slot_val[s] += 16
                o["target"] = slot_val[s]
        needed = [False] * n
        for i in range(n):
            for j in deps[i]:
                needed[j] = True
        last_on = {e: None for e in ENGS}
        for i, o in enumerate(ops):
            if o["barrier"]:
                o["b_last"] = dict(last_on)
                for e in ENGS:
                    if last_on[e] is not None:
                        needed[last_on[e]] = True
            elif not o["dma"]:
                last_on[o["eng"]] = i
        cnt = {e: 0 for e in ENGS}
        cur_slot_target = [0] * N_DMA_SEMS
        for i, o in enumerate(ops):
            if o["barrier"]:
                o["b_cnt"] = {e: (ops[j]["count"] if j is not None else 0) for e, j in o["b_last"].items()}
                o["b_dma"] = list(cur_slot_target)
            elif o["dma"]:
                if not o["d2d"]:
                    cur_slot_target[o["slot"]] = o["target"]
            elif needed[i]:
                cnt[o["eng"]] += 1
                o["count"] = cnt[o["eng"]]
        per_eng = {e: [] for e in ENGS}
        for i, o in enumerate(ops):
            if o["barrier"]:
                for e in ENGS:
                    per_eng[e].append(i)
            else:
                per_eng[o["eng"]].append(i)
        self.stats = {e: len(per_eng[e]) for e in ENGS}
        with contextlib.ExitStack() as es:
            esem = {e: es.enter_context(nc.semaphore("s_" + e)) for e in ENGS}
            dsem = [es.enter_context(nc.semaphore("d_%d" % s)) for s in range(N_DMA_SEMS)]
            block = es.enter_context(nc.Block())
            nwaits = [0]

            def run(ename, eng):
                waited = {}

                def need(key, val):
                    if val <= 0 or waited.get(key, 0) >= val:
                        return
                    waited[key] = val
                    sem = dsem[key[1]] if key[0] == "d" else esem[key[1]]
                    eng.wait_ge(sem, val)
                    nwaits[0] += 1

                for i in per_eng[ename]:
                    o = ops[i]
                    if o["barrier"]:
                        for e in ENGS:
                            need(("e", e), o["b_cnt"][e])
                        for s in range(N_DMA_SEMS):
                            need(("d", s), o["b_dma"][s])
                        continue
                    reqs = {}
                    for j in deps[i]:
                        a = ops[j]
                        if a["dma"]:
                            key = ("d", a["slot"]); val = a["target"]
                        else:
                            key = ("e", a["eng"]); val = a["count"]
                        if reqs.get(key, 0) < val:
                            reqs[key] = val
                    for key, val in sorted(reqs.items()):
                        need(key, val)
                    ins = o["fn"](eng)
                    if o["dma"]:
                        ins.then_inc(dsem[o["slot"]], 16)
                    elif needed[i]:
                        ins.then_inc(esem[ename], 1)
                for i in per_eng[ename]:
                    o = ops[i]
                    if o["dma"] and o["final"]:
                        need(("d", o["slot"]), o["target"])

            @block.tensor
            def _(eng):
                run("pe", eng)

            @block.scalar
            def _(eng):
                run("act", eng)

            @block.vector
            def _(eng):
                run("dve", eng)

            @block.gpsimd
            def _(eng):
                run("pool", eng)

            @block.sync
            def _(eng):
                run("sp", eng)
            self.stats["waits"] = nwaits[0]


class Arena:
    def __init__(self, nc, es, nbytes):
        self.t = es.enter_context(nc.sbuf_tensor("arena", [128, nbytes // 2], BF16))
        self.off = 0
        self.peak = 0
        self.cap = nbytes

    def alloc(self, nelem, dt):
        nb = nelem * (4 if dt in (F32, I32) else 2)
        nb = (nb + 63) // 64 * 64
        s = self.off
        self.off += nb
        self.peak = max(self.peak, self.off)
        assert self.off <= self.cap, ("arena overflow", self.off, self.cap)
        ap = self.t[:, s // 2:(s + nb) // 2]
        if dt in (F32, I32):
            ap = ap.bitcast(dt)
        return ap[:, 0:nelem]


def perm2(ap2, d):
    if d == 1:
        return ap2[:, None, :]
    return ap2.rearrange("p (i r) -> p r i", r=d)


def perm_tile(ap2, d, t):
    if d == 1:
        return ap2[:, t * 512:(t + 1) * 512]
    v = perm2(ap2, d)
    if d == 4:
        return v[:, t, :]
    return v[:, 4 * t:4 * t + 4, :]


def perm_block(ap2, d, bp):
    nb = 16 // d
    r, jb = bp // nb, bp % nb
    if d == 1:
        return ap2[:, jb * 128:(jb + 1) * 128]
    return perm2(ap2, d)[:, r, jb * 128:(jb + 1) * 128]


SETS = [
    dict(name="B1", d=1, nq=2, nk=2, H=4, base=1280, sink=False, gi=0, rows=128),
    dict(name="B2", d=4, nq=2, nk=2, H=4, base=1280 + 768, sink=False, gi=1, rows=512),
    dict(name="B3", d=16, nq=2, nk=2, H=4, base=1280 + 1536, sink=False, gi=2, rows=2048),
    dict(name="A", d=1, nq=8, nk=1, H=2, base=0, sink=True, gi=-1, rows=128),
]
GATE_COL = 3584


def build_program(n_seq=2, n_samp=16, do_prompt=True, do_sample=True, stages="123", dbg=None, sets=None):
    nc = bass.Bass("TRN2", target_bir_lowering=False)

    def dram(name, shape, dt, kind="Internal"):
        return nc.dram_tensor(name, list(shape), dt, kind=kind).ap()

    IN, OUT = "ExternalInput", "ExternalOutput"
    xp = dram("xp", [n_seq, SEQ, D_MODEL], F32, IN)
    w_in_d = dram("w_in", [D_MODEL, N_IN], F32, IN)
    w_pa_d = dram("w_pa", [1024, 1024], F32, IN)
    w_pb_d = dram("w_pb", [256, 1024], F32, IN)
    w_o_d = dram("w_o", [1024, 1024], F32, IN)
    w_gu_d = dram("w_gu", [1024, 2 * D_FF], F32, IN)
    w_dn_d = dram("w_dn", [D_FF, 1024], F32, IN)
    g1_d = dram("g1", [1, 1024], F32, IN)
    g2_d = dram("g2", [1, 1024], F32, IN)
    gf_d = dram("gf", [1, 1024], F32, IN)
    sinks_d = dram("sinks", [1, 16], F32, IN)
    c_ident_bf = dram("c_ident_bf", [128, 128], BF16, IN)
    c_ident_f = dram("c_ident_f", [128, 128], F32, IN)
    c_mask = dram("c_mask", [128, 512], BF16, IN)
    c_rot = dram("c_rot", [128, 128], BF16, IN)
    c_cos = dram("c_cos", [128, SEQ], F32, IN)
    c_sin = dram("c_sin", [128, SEQ], F32, IN)
    c_sinkl = dram("c_sinkl", [1, 128], BF16, IN)

    yp = dram("yp", [n_seq, SEQ, D_MODEL], F32, OUT)
    pa_o = dram("pa", [n_seq, 128, 2, 2, 64], F32, OUT)
    NS = n_samp
    xs_d = dram("xs", [NS, D_MODEL], F32, IN)
    ca_d = dram("ca", [NS, 128, 2, 2, 64], F32, IN)
    cb_d = [dram("cb1", [NS, 128, 2, 4, 64], F32, IN), dram("cb2", [NS, 512, 2, 4, 64], F32, IN),
            dram("cb3", [NS, 2048, 2, 4, 64], F32, IN)]
    c_sel = dram("c_sel", [16, 16 * 128], BF16, IN)
    c_eye16 = dram("c_eye16", [16, 16], F32, IN)
    c_cs = dram("c_cs", [16, 128], F32, IN)
    ys_o = dram("ys", [NS, D_MODEL], F32, OUT)
    sa_o = dram("sa", [NS, 128, 2, 2, 64], F32, OUT)
    sb_o = [dram("sb1", [NS, 128, 2, 4, 64], F32, OUT), dram("sb2", [NS, 512, 2, 4, 64], F32, OUT),
            dram("sb3", [NS, 2048, 2, 4, 64], F32, OUT)]
    pb_o = [dram("pb1", [n_seq, 128, 2, 4, 64], F32, OUT),
            dram("pb2", [n_seq, 512, 2, 4, 64], F32, OUT),
            dram("pb3", [n_seq, 2048, 2, 4, 64], F32, OUT)]

    w_in_b = dram("w_in_b", [D_MODEL, N_IN], BF16)
    w_pa_b = dram("w_pa_b", [1024, 1024], BF16)
    w_pb_b = dram("w_pb_b", [256, 1024], BF16)
    w_o_b = dram("w_o_b", [1024, 1024], BF16)
    w_gu_b = dram("w_gu_b", [1024, 2 * D_FF], BF16)
    w_dn_b = dram("w_dn_b", [D_FF, 1024], BF16)

    P = Prog(nc)
    with contextlib.ExitStack() as es:
        A = Arena(nc, es, 207 * 1024)
        pbank = [es.enter_context(nc.psum_tensor("pbank%d" % i, [128, 512], F32)) for i in range(8)]

        def PB(i):
            return pbank[i][:]

        def PK(i):
            return "psum%d" % i

        ident_bf = A.alloc(128, BF16)
        ident_f = A.alloc(128, F32)
        maskT = A.alloc(512, BF16)
        rotT = A.alloc(128, BF16)
        g1b = A.alloc(1024, F32)
        g2b = A.alloc(1024, F32)
        gfb = A.alloc(1024, F32)
        sinkl = A.alloc(128, BF16)
        es16 = A.alloc(16, F32)
        esrow = A.alloc(16 * 128, BF16)
        epst = A.alloc(1, F32)
        stats = A.alloc(64, F32)
        junk = A.alloc(1024, BF16)
        NW = 4
        wbuf = [A.alloc(8 * 512, BF16).rearrange("p (c n) -> p c n", c=8) for _ in range(NW)]
        m0 = A.off
        bufX = A.alloc(8192, F32)
        obacc = bufX.rearrange("p (s t) -> p s t", s=4)
        oaT = bufX.bitcast(BF16).rearrange("p (c t) -> p c t", c=8)
        obT = A.alloc(2 * SEQ, BF16).rearrange("p (c t) -> p c t", c=2)
        m1 = A.off
        hT = A.alloc(8 * SEQ, BF16).rearrange("p (c t) -> p c t", c=8)
        e1 = A.off
        A.off = m1
        gatesT = A.alloc(16 * 512, BF16).rearrange("p (c t) -> p c t", c=16)
        mT = A.alloc(8 * 512, BF16).rearrange("p (c t) -> p c t", c=8)
        actT = A.alloc(8 * 512, BF16).rearrange("p (c t) -> p c t", c=8)
        A.off = max(A.off, e1)
        m3 = A.off
        cosT = A.alloc(SEQ, F32)
        sinT = A.alloc(SEQ, F32)
        zb = [A.alloc(512, BF16) for _ in range(2)]
        t1 = [A.alloc(512, F32) for _ in range(2)]
        t2 = [A.alloc(512, F32) for _ in range(2)]
        ksum = [A.alloc(512, F32) for _ in range(2)]
        rden = [A.alloc(512, F32) for _ in range(2)]
        KT = A.alloc(2 * SEQ, BF16).rearrange("p (c t) -> p c t", c=2)
        Vaug = A.alloc(16 * 4 * 128, BF16).rearrange("p (b h e) -> p b h e", b=16, h=4)
        PT = [A.alloc(512, BF16) for _ in range(4)]
        stgV = [A.alloc(256, F32) for _ in range(2)]
        m3b = A.off
        QT = [A.alloc(4 * 512, BF16).rearrange("p (c t) -> p c t", c=4) for _ in range(2)]
        stgK = A.alloc(4 * 256, F32).rearrange("p (j f) -> p j f", j=4)
        e3b = A.off
        A.off = m3
        x1t = [A.alloc(1024, F32) for _ in range(4)]
        hn1 = [A.alloc(1024, BF16) for _ in range(4)]
        e3 = max(A.off, e3b)
        A.off = m3
        xts = [A.alloc(4 * 1024, F32).rearrange("p (j f) -> p j f", j=4) for _ in range(2)]
        hn = [A.alloc(1024, BF16) for _ in range(4)]
        aT = A.alloc(22 * 512, BF16).rearrange("p (c t) -> p c t", c=22)
        sgt = [A.alloc(512, F32) for _ in range(2)]
        tmpA = [A.alloc(512, F32) for _ in range(2)]
        tmpB = [A.alloc(512, F32) for _ in range(2)]
        yst = [A.alloc(1024, F32) for _ in range(2)]
        A.off = max(A.off, e3)
        e_all = A.off
        A.off = m0
        S_xs = A.alloc(1024, F32)
        S_hn = A.alloc(1024, BF16)
        S_hT = A.alloc(8 * 16, BF16).rearrange("p (c t) -> p c t", c=8)
        S_z = A.alloc(N_IN, F32)
        S_t1 = A.alloc(1152, F32)
        S_t2 = A.alloc(1152, F32)
        S_zq = A.alloc(3584, BF16)
        S_g = A.alloc(2048, F32)
        S_new = [A.alloc(2 * 2 * 64, F32)] + [A.alloc(2 * 4 * 64, F32) for _ in range(3)]
        S_vnew = A.alloc(14 * 128, BF16).rearrange("p (h e) -> p h e", h=14)
        S_pn = A.alloc(28 * 64, F32)
        S_pnew = A.alloc(28, F32)
        S_pd = A.alloc(16 * 28, BF16).rearrange("p (b h) -> p b h", b=16)
        S_sel = A.alloc(16 * 128, BF16).rearrange("p (b m) -> p b m", b=16)
        S_eye = A.alloc(16, F32)
        S_cs = A.alloc(128, F32)
        S_kv = [[A.alloc(256 if i == 0 else 512, F32) for i in range(4)] for _ in range(2)]
        S_va = [[A.alloc(4 * 128, BF16).rearrange("p (h e) -> p h e", h=4) for i in range(4)] for _ in range(2)]
        S_prod = A.alloc(1024, F32)
        S_st = A.alloc(32, F32)
        S_pt = [A.alloc(32, BF16) for _ in range(2)]
        S_rden = A.alloc(256, F32)
        S_oaT = A.alloc(8 * 16, BF16).rearrange("p (c t) -> p c t", c=8)
        S_obT = A.alloc(2 * 16, BF16).rearrange("p (c t) -> p c t", c=2)
        S_pa = A.alloc(1024, F32)
        S_pb = A.alloc(1024, F32)
        S_mbf = A.alloc(1024, BF16)
        S_mT = A.alloc(8 * 16, BF16).rearrange("p (c t) -> p c t", c=8)
        S_abf = A.alloc(D_FF, BF16)
        S_aT = A.alloc(22 * 16, BF16).rearrange("p (c t) -> p c t", c=22)
        S_y = A.alloc(1024, F32)
        assert A.off <= e_all, ("sample phase overflows aliased region", A.off, e_all)
        A.off = e_all
        print("arena bytes used per partition:", A.off, "peak", A.peak)

        rr = {}

        def rot(name, n):
            v = rr.get(name, 0)
            rr[name] = v + 1
            return v % n

        def dma(q, out, in_, reads, writes, final=False, bulk=False, d2d=False):
            P.op(q, lambda e: e.dma_start(out=out, in_=in_), reads=reads, writes=writes, dma=True,
                 final=final, bulk=bulk, d2d=(d2d or bulk))

        for (dst, src, key) in [(ident_bf, c_ident_bf, "ident_bf"), (ident_f, c_ident_f, "ident_f"),
                                (maskT, c_mask, "maskT"), (rotT, c_rot, "rotT")]:
            dma("sp", dst, src, [], [key])
        for (dst, src, key) in [(g1b, g1_d, "g1b"), (g2b, g2_d, "g2b"), (gfb, gf_d, "gfb")]:
            dma("sp", dst, src.broadcast_to([128, 1024]), [], [key])
        dma("sp", sinkl[0:1, :], c_sinkl, [], ["sinkl"])
        dma("sp", es16[0:1, :], sinks_d, [], ["es16"])
        P.op("act", lambda e: e.activation(out=es16[0:1, :], in_=es16[0:1, :], func=AF.Exp),
             reads=["es16"], writes=["es16"])
        P.op("dve", lambda e: e.tensor_copy(esrow[0:1, :].rearrange("p (h n) -> p h n", h=16),
                                            es16[0:1, :][:, :, None].broadcast_to([1, 16, 128])),
             reads=["es16"], writes=["esrow"])
        P.op("pool", lambda e: e.memset(epst, EPS), writes=["epst"])

        wkeys = {}
        W_IN_BLOCKS = [(1280, 2048), (2048, 2816), (2816, 3584), (0, 1280), (3584, 4608), (4608, 5632)]

        def cast_w_in():
            prev = []
            for bi, (c0, c1) in enumerate(W_IN_BLOCKS):
                k2 = "w_in_b_%d" % c0
                deps_ = []
                if bi in (1, 2):
                    deps_ = ["w_in_b_%d" % W_IN_BLOCKS[0][0]]
                elif bi == 3:
                    deps_ = ["w_in_b_%d" % W_IN_BLOCKS[1][0], "w_in_b_%d" % W_IN_BLOCKS[2][0]]
                dma("pool", w_in_b[:, c0:c1], w_in_d[:, c0:c1], deps_, [k2], d2d=True)
            wkeys["w_in_b"] = None

        def cast_w(src, dst, key):
            ncols = src.shape[1]
            c = 1408 if ncols == 5632 else 1024
            s2 = src.rearrange("k (a c) -> (k a) c", c=c)
            d2 = dst.rearrange("k (a c) -> (k a) c", c=c)
            nrows = s2.shape[0]
            step = 1024
            keys = []
            for r0 in range(0, nrows, step):
                r1 = min(nrows, r0 + step)
                k2 = "%s_%d" % (key, r0)
                dma("pool", d2[r0:r1, :], s2[r0:r1, :], [], [k2], d2d=True)
                keys.append(k2)
            wkeys[key] = keys

        cast_w_in()

        def cast_rest():
            if "w_pa_b" in wkeys:
                return
            cast_w(w_pa_d, w_pa_b, "w_pa_b")
            cast_w(w_pb_d, w_pb_b, "w_pb_b")
            cast_w(w_o_d, w_o_b, "w_o_b")
            cast_w(w_gu_d, w_gu_b, "w_gu_b")
            cast_w(w_dn_d, w_dn_b, "w_dn_b")

        cast_rest()
        prefetched = {}

        def load_w_c(tag, *args):
            if tag in prefetched:
                return prefetched.pop(tag)
            return load_w(*args)

        def prefetch_w(tag, *args):
            prefetched[tag] = load_w(*args)

        def wdeps(key, c0, ncols):
            if key == "w_in_b":
                return ["w_in_b_%d" % a for (a, b) in W_IN_BLOCKS if a < c0 + ncols and c0 < b]
            return wkeys[key]

        def load_w(src_b, key, r0, nrows, c0, ncols):
            i = rot("wbuf", NW)
            kc = nrows // 128
            src = src_b[r0:r0 + nrows, c0:c0 + ncols].rearrange("(c p) n -> p c n", p=128)
            dma("sp", wbuf[i][:, 0:kc, 0:ncols], src, wdeps(key, c0, ncols), ["wbuf%d" % i])
            return wbuf[i], "wbuf%d" % i

        def rms_norm(x_ap, x_keys, gb, gkey, out_ap, out_keys, npart=128):
            si = rot("stats", 16)
            ss = stats[0:npart, 4 * si:4 * si + 1]
            ms = stats[0:npart, 4 * si + 1:4 * si + 2]
            rs = stats[0:npart, 4 * si + 2:4 * si + 3]
            sk = "stats%d" % si
            P.op("act", lambda e: e.activation(out=junk[0:npart, :], in_=x_ap, func=AF.Square,
                                               accum_out=ss),
                 reads=x_keys, writes=["junk", sk])
            P.op("act", lambda e: e.activation(out=ms, in_=ss, func=AF.Sqrt, bias=epst[0:npart, :],
                                               scale=1.0 / D_MODEL), reads=[sk, "epst"], writes=[sk])
            P.op("dve", lambda e: e.reciprocal(rs, ms), reads=[sk], writes=[sk])
            P.op("dve", lambda e: e.scalar_tensor_tensor(out=out_ap, in0=x_ap, scalar=rs, in1=gb[0:npart, :],
                                                         op0=ALU.mult, op1=ALU.mult),
                 reads=list(x_keys) + [sk, gkey], writes=out_keys)

        def transpose_to(hn_ap, hn_key, dst3, dst_key, col0, ncols=128, npart=128, bank=4):
            bk = bank
            pv = PB(bk).bitcast(BF16).rearrange("p (c t) -> p c t", c=8)
            for c in range(8):
                P.op("pe", lambda e, c=c: e.transpose(pv[:, c, 0:npart], hn_ap[:, c * 128:(c + 1) * 128],
                                                      ident_bf[0:npart, 0:npart]),
                     reads=[hn_key, "ident_bf"], writes=[PK(bk)])
            P.op("act", lambda e: e.copy(dst3[:, 0:8, col0:col0 + npart], pv[:, :, 0:npart]),
                 reads=[PK(bk)], writes=[dst_key])

        def rope_parts(bk, cos_ap, sin_ap, out_bf, out_key, ks=None, ks_key=None, shape4=False, after=None):
            i = rot("rope", 2)

            def v(ap):
                return ap.rearrange("p (r i) -> p r i", r=4) if shape4 else ap

            def part1():
                P.op("act", lambda e: e.copy(zb[i], PB(bk)), reads=[PK(bk)], writes=["zb%d" % i])
                P.op("dve", lambda e: e.tensor_tensor(v(t1[i]), v(PB(bk)), cos_ap, op=ALU.mult),
                     reads=[PK(bk), "cosT"], writes=["t1_%d" % i])

            def part2():
                P.op("pe", lambda e: e.matmul(PB(2), rotT, zb[i], start=True, stop=True),
                     reads=["rotT", "zb%d" % i], writes=[PK(2)])
                P.op("dve", lambda e: e.tensor_tensor(v(t2[i]), v(PB(2)), sin_ap, op=ALU.mult),
                     reads=[PK(2), "sinT"], writes=["t2_%d" % i])
                if ks is None:
                    P.op("pool", lambda e: e.tensor_tensor(out_bf, t1[i], t2[i], op=ALU.add),
                         reads=["t1_%d" % i, "t2_%d" % i], writes=[out_key])
                else:
                    P.op("pool", lambda e: e.tensor_tensor(ks, t1[i], t2[i], op=ALU.add),
                         reads=["t1_%d" % i, "t2_%d" % i], writes=[ks_key])
                    P.op("act", lambda e: e.copy(out_bf, ks), reads=[ks_key], writes=[out_key])
                if after is not None:
                    after()
            return part1, part2

        def stage1(s):
            def nrm(b):
                i = b % 4
                dma("sp", x1t[i], xp[s, b * 128:(b + 1) * 128, :], [], ["x1t%d" % i])
                rms_norm(x1t[i], ["x1t%d" % i], g1b, "g1b", hn1[i], ["hn1_%d" % i])
            for b in range(3):
                nrm(b)
            for b in range(16):
                if b + 3 < 16:
                    nrm(b + 3)
                transpose_to(hn1[b % 4], "hn1_%d" % (b % 4), hT, "hT%d" % b, b * 128, bank=3 + (b % 2))
                if b == 13:
                    st0 = SETS[0]
                    b0 = st0["base"]
                    prefetch_w((s, st0["name"], "k"), w_in_b, "w_in_b", 0, 1024, b0 + st0["nq"] * 128, st0["nk"] * 128)
                    prefetch_w((s, st0["name"], "v"), w_in_b, "w_in_b", 0, 1024,
                               b0 + st0["nq"] * 128 + st0["nk"] * 128, st0["H"] * 64)
                    prefetch_w((s, st0["name"], "q", 0), w_in_b, "w_in_b", 0, 1024, b0, min(512, st0["nq"] * 128))

        HTK = ["hT%d" % b_ for b_ in range(16)]
        deferred_final = []

        def stage2(s):
            dma("sp", cosT, c_cos, [], ["cosT"])
            dma("sp", sinT, c_sin, [], ["sinT"])
            P.op("dve", lambda e: e.memset(Vaug[:, :, :, 64:128], 1.0), writes=["Vaug_%d" % b_ for b_ in range(16)])
            def do_set(st):
                d, nq, nk, H, base, name = st["d"], st["nq"], st["nk"], st["H"], st["base"], st["name"]
                nb = 16 // d
                qcol, kcol, vcol = base, base + nq * 128, base + nq * 128 + nk * 128
                ncv = H * 64
                is_a = st["sink"]
                rows = st["rows"]
                out_t = pa_o if is_a else pb_o[st["gi"]]
                issue_bulk(1)
                mark("  set " + name)

                def state_block(bp):
                    r, jb = bp // nb, bp % nb
                    t0 = r + d * 128 * jb
                    if t0 < SEQ - rows:
                        return None
                    start = t0 - (SEQ - rows)
                    return slice(start, start + d * 127 + 1, d)

                def inproj(w, wkey, c0, t, bk):
                    for c8 in range(8):
                        P.op("pe", lambda e, c8=c8: e.matmul(
                            PB(bk) if d < 16 else PB(bk).rearrange("p (r i) -> p r i", r=4),
                            w[:, c8, c0:c0 + 128], perm_tile(hT[:, c8, :], d, t),
                            start=(c8 == 0), stop=(c8 == 7)), reads=[wkey] + HTK, writes=[PK(bk)])

                wk, wkk = load_w_c((s, name, "k"), w_in_b, "w_in_b", 0, 1024, kcol, nk * 128)
                pend = [None]

                def flush():
                    if pend[0] is not None:
                        pend[0]()
                        pend[0] = None

                for t in range(4):
                    for kc in range(nk):
                        bk = rot("inbank", 2)
                        inproj(wk, wkk, kc * 128, t, bk)
                        ki = rot("ksum", 2)

                        def kstate(t=t, kc=kc, ki=ki):
                            for j in range(4):
                                sl = state_block(4 * t + j)
                                if sl is None:
                                    continue
                                P.op("pe", lambda e, j=j: e.transpose(
                                    PB(3)[:, 0:128], ksum[ki][:, j * 128:(j + 1) * 128], ident_f),
                                    reads=["ksum%d" % ki, "ident_f"], writes=[PK(3)])
                                P.op("dve", lambda e, j=j: e.tensor_copy(
                                    stgK[:, j, kc * 128:(kc + 1) * 128], PB(3)[:, 0:128]),
                                    reads=[PK(3)], writes=["stgK%d" % j])
                        p1, p2 = rope_parts(bk, perm_tile(cosT, d, t), perm_tile(sinT, d, t),
                                            KT[:, kc, t * 512:(t + 1) * 512], "KT",
                                            ks=ksum[ki], ks_key="ksum%d" % ki, shape4=(d == 16), after=kstate)
                        p1()
                        flush()
                        pend[0] = p2
                    if any(state_block(4 * t + j) is not None for j in range(4)):
                        flush()
                        for j in range(4):
                            sl = state_block(4 * t + j)
                            if sl is None:
                                continue
                            dma("act", out_t[s, sl, 0, :, :],
                                stgK[:, j, 0:ncv].rearrange("p (h e) -> p h e", h=H),
                                ["stgK%d" % j], ["out_" + name], final=True)
                flush()
                wv, wvk = load_w_c((s, name, "v"), w_in_b, "w_in_b", 0, 1024, vcol, ncv)
                for bp in range(16):
                    vb = 3 - (bp % 2)
                    for c8 in range(8):
                        P.op("pe", lambda e, c8=c8, bp=bp, vb=vb: e.matmul(
                            PB(vb)[:, 0:ncv], perm_block(hT[:, c8, :], d, bp), wv[:, c8, 0:ncv],
                            start=(c8 == 0), stop=(c8 == 7)), reads=[wvk] + HTK, writes=[PK(vb)])
                    P.op("act", lambda e, bp=bp, vb=vb: e.copy(
                        Vaug[:, bp, 0:H, 0:64], PB(vb)[:, 0:ncv].rearrange("p (h e) -> p h e", h=H)),
                        reads=[PK(vb)], writes=["Vaug_%d" % bp])
                    if deferred_final:
                        deferred_final.pop(0)()
                    sl = state_block(bp)
                    if sl is not None:
                        vi = rot("stgV", 2)
                        P.op("dve", lambda e, vi=vi, vb=vb: e.tensor_copy(stgV[vi][:, 0:ncv], PB(vb)[:, 0:ncv]),
                             reads=[PK(vb)], writes=["stgV%d" % vi])
                        dma("act", out_t[s, sl, 1, :, :],
                            stgV[vi][:, 0:ncv].rearrange("p (h e) -> p h e", h=H),
                            ["stgV%d" % vi], ["out_" + name], final=True)
                wqs = []
                for q0 in range(0, nq * 128, 512):
                    wqs.append(load_w_c((s, name, "q", q0), w_in_b, "w_in_b", 0, 1024, qcol + q0, min(512, nq * 128 - q0)))
                ST_BANKS = [4, 5, 3]
                LA = 2

                def scores(Pr):
                    T0 = Pr[0]
                    sb, hp, pi = T0["sb"], T0["has_prev"], T0["pi"]
                    w = 256 * len(Pr)
                    P.op("pe", lambda e: e.matmul(
                        PB(sb)[:, 0:w], ident_bf, maskT[:, 0:w], start=True, stop=False),
                        reads=["ident_bf", "maskT"], writes=[PK(sb)])
                    for hi, T in enumerate(Pr):
                        hs, bp, j, qi, ql, kc = T["hs"], T["bp"], T["j"], T["qi"], T["ql"], T["kc"]
                        qk = "QT%d" % qi
                        off = 256 * hi
                        last = (hi == len(Pr) - 1)
                        if hp:
                            P.op("pe", lambda e, T=T, off=off, hs=hs, bp=bp, j=j, qi=qi, ql=ql, kc=kc: e.matmul(
                                PB(sb)[:, off:off + 128], KT[hs, kc, (bp - 1) * 128:bp * 128],
                                QT[qi][hs, ql, j * 128:(j + 1) * 128], start=False, stop=False),
                                reads=["KT", qk], writes=[PK(sb)])
                        P.op("pe", lambda e, T=T, off=off, hs=hs, bp=bp, j=j, qi=qi, ql=ql, kc=kc, last=last: e.matmul(
                            PB(sb)[:, off + 128:off + 256], KT[hs, kc, bp * 128:(bp + 1) * 128],
                            QT[qi][hs, ql, j * 128:(j + 1) * 128], start=False, stop=last),
                            reads=["KT", qk], writes=[PK(sb)])
                    if hp:
                        P.op("act", lambda e: e.activation(
                            out=PT[pi][:, 0:w], in_=PB(sb)[:, 0:w], func=AF.Exp, scale=0.125),
                            reads=[PK(sb)], writes=["PT%d" % pi])
                    else:
                        nh_ = len(Pr)
                        P.op("act", lambda e: e.activation(
                            out=PT[pi][:, 0:w].rearrange("p (h c) -> p h c", h=nh_)[:, :, 128:256],
                            in_=PB(sb)[:, 0:w].rearrange("p (h c) -> p h c", h=nh_)[:, :, 128:256],
                            func=AF.Exp, scale=0.125), reads=[PK(sb)], writes=["PT%d" % pi])

                def pv(Pr):
                    for hi, T in enumerate(Pr):
                        pob, slot, bp, vh, pi, hp = T["pob"], T["slot"], T["bp"], T["vh"], T["pi"], T["has_prev"]
                        off = 256 * hi
                        po = PB(pob)[:, slot * 128:(slot + 1) * 128]
                        first = True
                        if is_a and slot == 0:
                            hq0 = T["grp"][0][0] + 8 * T["grp"][0][1]
                            P.op("pe", lambda e, hq0=hq0, pob=pob: e.matmul(
                                PB(pob), sinkl[0:1, :], esrow[0:1, hq0 * 128:(hq0 + 4) * 128], start=True, stop=False),
                                reads=["sinkl", "esrow"], writes=[PK(pob)])
                        if is_a:
                            first = False
                        if hp:
                            P.op("pe", lambda e, po=po, bp=bp, vh=vh, pi=pi, off=off, first=first: e.matmul(
                                po, Vaug[:, bp - 1, vh, :], PT[pi][:, off:off + 128], start=first, stop=False),
                                reads=["Vaug_%d" % (bp - 1), "PT%d" % pi], writes=[PK(pob)])
                            first = False
                        P.op("pe", lambda e, po=po, bp=bp, vh=vh, pi=pi, off=off, first=first, T=T: e.matmul(
                            po, Vaug[:, bp, vh, :], PT[pi][:, off + 128:off + 256], start=first,
                            stop=(T["last"] or not is_a)),
                            reads=["Vaug_%d" % bp, "PT%d" % pi], writes=[PK(pob)])
                        if T["last"]:
                            evac(T)

                def evac(T):
                    pob, bp = T["pob"], T["bp"]
                    if is_a:
                        half, qc0 = T["grp"][0][1], T["grp"][0][0]
                        ri = rot("rden", 2)
                        P.op("act", lambda e: e.activation(out=rden[ri][0:64, :], in_=PB(pob)[64:128, :], func=AF.Ln),
                             reads=[PK(pob)], writes=["rden%d" % ri])
                        P.op("act", lambda e: e.activation(out=rden[ri][0:64, :], in_=rden[ri][0:64, :], func=AF.Exp,
                                                           scale=-1.0), reads=["rden%d" % ri], writes=["rden%d" % ri])
                        P.op("dve", lambda e: e.tensor_tensor(
                            oaT[half * 64:half * 64 + 64, qc0:qc0 + 4, bp * 128:(bp + 1) * 128],
                            PB(pob)[0:64, :].rearrange("p (c t) -> p c t", c=4),
                            rden[ri][0:64, :].rearrange("p (c t) -> p c t", c=4), op=ALU.mult),
                            reads=[PK(pob), "rden%d" % ri], writes=["oaT", "obacc"])
                    else:
                        r, jb = bp // nb, bp % nb
                        if d == 1:
                            ov = obacc[:, :, bp * 128:(bp + 1) * 128]
                        else:
                            ov = obacc.rearrange("p s (i r) -> p s r i", r=d)[:, :, r, jb * 128:(jb + 1) * 128]
                        pv4 = PB(pob).rearrange("p (c t) -> p c t", c=4)
                        if st["gi"] == 0:
                            P.op("act", lambda e: e.copy(ov, pv4), reads=[PK(pob)], writes=["obacc"])
                        else:
                            P.op("dve", lambda e: e.tensor_tensor(ov, pv4, ov, op=ALU.add),
                                 reads=[PK(pob), "obacc"], writes=["obacc"])

                units = [(t, qh) for t in range(4) for qh in range((nq + 3) // 4)]
                qis = {}

                def qproj(u):
                    t, qh = units[u]
                    qi = rot("QT", 2)
                    qis[u] = qi
                    nqc = min(4, nq - 4 * qh)
                    for ql in range(nqc):
                        qc = 4 * qh + ql
                        wq, wqk = wqs[qc // 4]
                        bk = rot("inbank", 2)
                        inproj(wq, wqk, (qc % 4) * 128, t, bk)
                        p1, p2 = rope_parts(bk, perm_tile(cosT, d, t), perm_tile(sinT, d, t),
                                            QT[qi][:, ql, :], "QT%d" % qi, shape4=(d == 16))
                        p1()
                        flush()
                        pend[0] = p2
                    flush()

                def attn(u):
                    t, qh = units[u]
                    qi = qis[u]
                    tasks = []
                    for j in range(4):
                        bp = 4 * t + j
                        has_prev = (bp % nb) != 0
                        if is_a:
                            groups = [[(qh * 4 + k, half) for k in range(4)] for half in range(2)]
                        else:
                            groups = [[(sl_ % 2, sl_ // 2) for sl_ in range(4)]]
                        for grp in groups:
                            pob = 6 + rot("pobank", 2)
                            for slot, (qc, half) in enumerate(grp):
                                tasks.append(dict(
                                    j=j, bp=bp, has_prev=has_prev, lo=0 if has_prev else 128, qc=qc, half=half,
                                    ql=qc - 4 * qh, kc=0 if is_a else qc, vh=half if is_a else 2 * qc + half,
                                    hs=slice(half * 64, half * 64 + 64), qi=qi, pob=pob, slot=slot, grp=grp,
                                    last=(slot == len(grp) - 1)))
                    pairs = [tasks[i:i + 2] for i in range(0, len(tasks), 2)]
                    for i, Pr in enumerate(pairs):
                        sb_ = ST_BANKS[rot("stbank", 3)]
                        pi_ = rot("PT", 4)
                        for T in Pr:
                            T["sb"] = sb_
                            T["pi"] = pi_
                        scores(Pr)
                        if i >= LA:
                            pv(pairs[i - LA])
                    for Pr in pairs[max(0, len(pairs) - LA):]:
                        pv(Pr)

                qproj(0)
                for u in range(len(units)):
                    if u + 1 < len(units):
                        qproj(u + 1)
                    attn(u)
                if st["gi"] == 2:
                    for tt in range(4):
                        for h in range(4):
                            def fin(tt=tt, h=h):
                                ri = rot("rden", 2)
                                P.op("act", lambda e: e.activation(
                                    out=rden[ri][0:64, :], in_=obacc[64:128, h, tt * 512:(tt + 1) * 512], func=AF.Ln),
                                    reads=["obacc"], writes=["rden%d" % ri])
                                P.op("act", lambda e: e.activation(
                                    out=rden[ri][0:64, :], in_=rden[ri][0:64, :], func=AF.Exp, scale=-1.0),
                                    reads=["rden%d" % ri], writes=["rden%d" % ri])
                                P.op("dve", lambda e: e.tensor_tensor(
                                    obT[(h // 2) * 64:(h // 2) * 64 + 64, h % 2, tt * 512:(tt + 1) * 512],
                                    obacc[0:64, h, tt * 512:(tt + 1) * 512], rden[ri][0:64, :], op=ALU.mult),
                                    reads=["obacc", "rden%d" % ri], writes=["obT"])
                            deferred_final.append(fin)

            for st in SETS:
                do_set(st)
            while deferred_final:
                deferred_final.pop(0)()
            if "3" in stages:
                for q in range(3):
                    prefetch_w((s, 0, "gate", q), w_in_b, "w_in_b", 0, 1024, GATE_COL + q * 512, 512)

        def stage3(s):
            def do_group(g):
                tok0 = g * 512
                mark("  group %d" % g)
                issue_bulk(2)
                xt = xts[g % 2]
                XK = "xt%d_" % (g % 2)

                def load_and_norm1(gg):
                    xt_ = xts[gg % 2]
                    xk_ = "xt%d_" % (gg % 2)
                    for j in range(4):
                        dma("sp", xt_[:, j, :], xp[s, gg * 512 + j * 128:gg * 512 + (j + 1) * 128, :], [],
                            [xk_ + "%d" % j])
                    for j in range(4):
                        rms_norm(xt_[:, j, :], [xk_ + "%d" % j], g1b, "g1b", hn[j], ["hn%d" % j])
                if g == 0:
                    load_and_norm1(0)
                for j in range(4):
                    transpose_to(hn[j], "hn%d" % j, actT, "actT", j * 128, bank=4 + (j % 2))
                for q in range(4):
                    wg, wgk = load_w_c((s, g, "gate", q), w_in_b, "w_in_b", 0, 1024, GATE_COL + q * 512, 512)
                    for k in range(4):
                        gc = q * 4 + k
                        bk = rot("fbank", 4)
                        for c8 in range(8):
                            P.op("pe", lambda e, c8=c8, k=k, bk=bk, wg=wg: e.matmul(
                                PB(bk), wg[:, c8, k * 128:(k + 1) * 128], actT[:, c8, :],
                                start=(c8 == 0), stop=(c8 == 7)), reads=[wgk, "actT"], writes=[PK(bk)])
                        P.op("act", lambda e, gc=gc, bk=bk: e.activation(
                            out=gatesT[:, gc, :], in_=PB(bk), func=AF.Sigmoid),
                            reads=[PK(bk)], writes=["gatesT"])
                wpa = [load_w(w_pa_b, "w_pa_b", 0, 1024, n * 512, 512) for n in range(2)]
                wpb = [load_w(w_pb_b, "w_pb_b", 0, 256, n * 512, 512) for n in range(2)]
                for oc in range(8):
                    bka = rot("fbank", 4)
                    w, wkey = wpa[oc // 4]
                    for c8 in range(8):
                        P.op("pe", lambda e, c8=c8, oc=oc, bka=bka, w=w: e.matmul(
                            PB(bka), w[:, c8, (oc % 4) * 128:(oc % 4 + 1) * 128], oaT[:, c8, tok0:tok0 + 512],
                            start=(c8 == 0), stop=(c8 == 7)), reads=[wkey, "oaT"], writes=[PK(bka)])
                    ia = rot("tmpA", 2)
                    P.op("dve", lambda e, ia=ia, bka=bka, oc=oc: e.tensor_tensor(
                        tmpA[ia], PB(bka), gatesT[:, oc, :], op=ALU.mult),
                        reads=[PK(bka), "gatesT"], writes=["tmpA%d" % ia])
                    bkb = rot("fbank", 4)
                    w, wkey = wpb[oc // 4]
                    for c2 in range(2):
                        P.op("pe", lambda e, c2=c2, oc=oc, bkb=bkb, w=w: e.matmul(
                            PB(bkb), w[:, c2, (oc % 4) * 128:(oc % 4 + 1) * 128], obT[:, c2, tok0:tok0 + 512],
                            start=(c2 == 0), stop=(c2 == 1)), reads=[wkey, "obT"], writes=[PK(bkb)])
                    ib = rot("tmpB", 2)
                    P.op("dve", lambda e, ib=ib, bkb=bkb, oc=oc: e.tensor_tensor(
                        tmpB[ib], PB(bkb), gatesT[:, 8 + oc, :], op=ALU.mult),
                        reads=[PK(bkb), "gatesT"], writes=["tmpB%d" % ib])
                    P.op("pool", lambda e, ia=ia, ib=ib, oc=oc: e.tensor_tensor(
                        mT[:, oc, :], tmpA[ia], tmpB[ib], op=ALU.add),
                        reads=["tmpA%d" % ia, "tmpB%d" % ib], writes=["mT"])
                wo = [load_w(w_o_b, "w_o_b", 0, 1024, n * 512, 512) for n in range(2)]
                for j in range(4):
                    for n in range(2):
                        bk = rot("fbank", 4)
                        w, wkey = wo[n]
                        for c8 in range(8):
                            P.op("pe", lambda e, c8=c8, j=j, bk=bk, w=w: e.matmul(
                                PB(bk), mT[:, c8, j * 128:(j + 1) * 128], w[:, c8, :],
                                start=(c8 == 0), stop=(c8 == 7)), reads=[wkey, "mT"], writes=[PK(bk)])
                        P.op("dve", lambda e, j=j, n=n, bk=bk: e.tensor_tensor(
                            xt[:, j, n * 512:(n + 1) * 512], PB(bk), xt[:, j, n * 512:(n + 1) * 512], op=ALU.add),
                            reads=[PK(bk), XK + "%d" % j], writes=[XK + "%d" % j])
                for j in range(4):
                    rms_norm(xt[:, j, :], [XK + "%d" % j], g2b, "g2b", hn[j], ["hn%d" % j])
                for j in range(4):
                    transpose_to(hn[j], "hn%d" % j, actT, "actT", j * 128, bank=4 + (j % 2))
                if g + 1 < 4:
                    load_and_norm1(g + 1)
                for q in range(6):
                    ncol = 512 if q < 5 else 256
                    wgt, wgtk = load_w(w_gu_b, "w_gu_b", 0, 1024, q * 512, ncol)
                    wup, wupk = load_w(w_gu_b, "w_gu_b", 0, 1024, D_FF + q * 512, ncol)
                    for k in range(ncol // 128):
                        fc = q * 4 + k
                        bg = 4 + rot("gbank", 2)
                        bu = 6 + rot("ubank", 2)
                        for c8 in range(8):
                            P.op("pe", lambda e, c8=c8, k=k, bg=bg, wgt=wgt: e.matmul(
                                PB(bg), wgt[:, c8, k * 128:(k + 1) * 128], actT[:, c8, :],
                                start=(c8 == 0), stop=(c8 == 7)), reads=[wgtk, "actT"], writes=[PK(bg)])
                        for c8 in range(8):
                            P.op("pe", lambda e, c8=c8, k=k, bu=bu, wup=wup: e.matmul(
                                PB(bu), wup[:, c8, k * 128:(k + 1) * 128], actT[:, c8, :],
                                start=(c8 == 0), stop=(c8 == 7)), reads=[wupk, "actT"], writes=[PK(bu)])
                        si = rot("sgt", 2)
                        P.op("act", lambda e, si=si, bg=bg: e.activation(out=sgt[si], in_=PB(bg), func=AF.Silu),
                             reads=[PK(bg)], writes=["sgt%d" % si])
                        P.op("dve", lambda e, si=si, bu=bu, fc=fc: e.tensor_tensor(
                            aT[:, fc, :], PB(bu), sgt[si], op=ALU.mult),
                            reads=[PK(bu), "sgt%d" % si], writes=["aT"])
                for n in range(2):
                    for wi, (f0, nf) in enumerate([(0, 8), (8, 8), (16, 6)]):
                        w, wkey = load_w(w_dn_b, "w_dn_b", f0 * 128, nf * 128, n * 512, 512)
                        for j in range(4):
                            for f in range(nf):
                                fc = f0 + f
                                P.op("pe", lambda e, j=j, f=f, fc=fc, w=w: e.matmul(
                                    PB(j), aT[:, fc, j * 128:(j + 1) * 128], w[:, f, :],
                                    start=(fc == 0), stop=(fc == 21)), reads=[wkey, "aT"], writes=[PK(j)])
                    for j in range(4):
                        P.op("dve", lambda e, j=j, n=n: e.tensor_tensor(
                            xt[:, j, n * 512:(n + 1) * 512], PB(j), xt[:, j, n * 512:(n + 1) * 512], op=ALU.add),
                            reads=[PK(j), XK + "%d" % j], writes=[XK + "%d" % j])
                for j in range(4):
                    yi = rot("yst", 2)
                    rms_norm(xt[:, j, :], [XK + "%d" % j], gfb, "gfb", yst[yi], ["yst%d" % yi])
                    dma("pool", yp[s, tok0 + j * 128:tok0 + (j + 1) * 128, :], yst[yi],
                        ["yst%d" % yi], ["yp"], final=True)

            for g in range(4):
                do_group(g)

        bulk_list = []

        def sample_copies():
            bulk_list.append((sa_o[:, 0:127], ca_d[:, 1:128], "sa_copy"))
            bulk_list.append((sb_o[0][:, 0:127], cb_d[0][:, 1:128], "sb1_copy"))
            for b in range(0, NS, 4):
                bulk_list.append((sb_o[1][b:b + 4, 0:511], cb_d[1][b:b + 4, 1:512], "sb2_copy%d" % b))
            for b in range(NS):
                bulk_list.append((sb_o[2][b, 0:2047], cb_d[2][b, 1:2048], "sb3_copy%d" % b))

        def issue_bulk(n):
            for _ in range(n):
                if bulk_list:
                    o_, i_, k_ = bulk_list.pop(0)
                    dma("sp", o_, i_, [], [k_], final=True, bulk=True)

        def tm_matmul(lhs3, lkey, nk, w_b, wkey, r0, c0, ncols, consume):
            for n0 in range(0, ncols, 512):
                n = min(512, ncols - n0)
                pieces = [(k0, min(8, nk - k0)) for k0 in range(0, nk, 8)]
                bk = rot("sbank", 4)
                for (k0, kn) in pieces:
                    w, wk_ = load_w(w_b, wkey, r0 + k0 * 128, kn * 128, c0 + n0, n)
                    for k in range(kn):
                        P.op("pe", lambda e, k=k, k0=k0, w=w, bk=bk, n=n: e.matmul(
                            PB(bk)[0:NS, 0:n], lhs3[:, k0 + k, 0:NS], w[:, k, 0:n],
                            start=(k0 + k == 0), stop=(k0 + k == nk - 1)),
                            reads=[lkey, wk_], writes=[PK(bk)])
                consume(bk, n0, n)

        def sample_phase():
            N = NS
            dma("sp", S_xs[0:N, :], xs_d, [], ["S_xs"])
            dma("sp", S_sel[0:16], c_sel.rearrange("p (b m) -> p b m", b=16), [], ["S_sel"])
            dma("sp", S_eye[0:16, :], c_eye16, [], ["S_eye"])
            dma("sp", S_cs[0:16, :], c_cs, [], ["S_cs"])
            for q in range(2):
                for i in range(4):
                    P.op("dve", lambda e, q=q, i=i: e.memset(S_va[q][i][:, :, 64:128], 1.0),
                         writes=["S_va%d_%d" % (q, i)])
            P.op("dve", lambda e: e.memset(S_vnew[0:N, :, 64:128], 1.0), writes=["S_vnew"])
            rms_norm(S_xs[0:N, :], ["S_xs"], g1b, "g1b", S_hn[0:N, :], ["S_hn"], npart=N)
            transpose_to(S_hn[0:N, :], "S_hn", S_hT, "S_hT", 0, npart=N)
            def ev_z(bk, c0, n):
                P.op("act", lambda e: e.copy(S_z[0:N, c0:c0 + n], PB(bk)[0:N, 0:n]), reads=[PK(bk)], writes=["S_z"])
            tm_matmul(S_hT, "S_hT", 8, w_in_b, "w_in_b", 0, 0, N_IN, ev_z)
            if SU_ == "inproj":
                return
            P.op("act", lambda e: e.activation(out=S_g[0:N, :], in_=S_z[0:N, GATE_COL:GATE_COL + 2048], func=AF.Sigmoid),
                 reads=["S_z"], writes=["S_g"])
            cosb = S_cs[0:N, 0:64]
            sinb = S_cs[0:N, 64:128]
            for (c0, nh) in [(0, 18), (1280, 8), (1280 + 768, 8), (1280 + 1536, 8)]:
                zv = S_z[0:N, c0:c0 + nh * 64].rearrange("p (h e) -> p h e", h=nh)
                a1 = S_t1[0:N, 0:nh * 64].rearrange("p (h e) -> p h e", h=nh)
                a2 = S_t2[0:N, 0:nh * 64].rearrange("p (h e) -> p h e", h=nh)
                P.op("dve", lambda e, zv=zv, a1=a1, nh=nh: e.tensor_tensor(
                    a1, zv, cosb[:, None, :].broadcast_to([N, nh, 64]), op=ALU.mult),
                    reads=["S_z", "S_cs"], writes=["S_t1"])
                P.op("dve", lambda e, zv=zv, a2=a2, nh=nh: e.tensor_tensor(
                    a2[:, :, 0:32], zv[:, :, 32:64], sinb[:, None, 0:32].broadcast_to([N, nh, 32]), op=ALU.mult),
                    reads=["S_z", "S_cs"], writes=["S_t2"])
                P.op("dve", lambda e, zv=zv, a2=a2, nh=nh: e.tensor_tensor(
                    a2[:, :, 32:64], zv[:, :, 0:32], sinb[:, None, 32:64].broadcast_to([N, nh, 32]), op=ALU.mult),
                    reads=["S_z", "S_cs"], writes=["S_t2"])
                P.op("dve", lambda e, zv=zv, a1=a1, a2=a2: e.tensor_tensor(zv, a1, a2, op=ALU.add),
                     reads=["S_t1", "S_t2"], writes=["S_z"])
            P.op("act", lambda e: e.copy(S_zq[0:N, :], S_z[0:N, 0:3584]), reads=["S_z"], writes=["S_zq"])
            sets_s = [dict(kc=1024, vc=1152, H=2, W=128, out=sa_o, hv0=0)]
            for g in range(3):
                base = 1280 + 768 * g
                sets_s.append(dict(kc=base + 256, vc=base + 512, H=4, W=[128, 512, 2048][g], out=sb_o[g], hv0=2 + 4 * g))
            for si, ss_ in enumerate(sets_s):
                H = ss_["H"]
                nv = S_new[si][0:N, :].rearrange("p (k f) -> p k f", k=2)
                P.op("act", lambda e, nv=nv, ss_=ss_, H=H: e.copy(nv[:, 0, :], S_z[0:N, ss_["kc"]:ss_["kc"] + H * 64]),
                     reads=["S_z"], writes=["S_new%d" % si])
                P.op("act", lambda e, nv=nv, ss_=ss_, H=H: e.copy(nv[:, 1, :], S_z[0:N, ss_["vc"]:ss_["vc"] + H * 64]),
                     reads=["S_z"], writes=["S_new%d" % si])
                dma("pool", ss_["out"][:, ss_["W"] - 1].rearrange("b k h e -> b (k h e)"), S_new[si][0:N, :],
                    ["S_new%d" % si], ["snew_out%d" % si], final=True)
                P.op("act", lambda e, ss_=ss_, H=H: e.copy(
                    S_vnew[0:N, ss_["hv0"]:ss_["hv0"] + H, 0:64],
                    S_z[0:N, ss_["vc"]:ss_["vc"] + H * 64].rearrange("p (h e) -> p h e", h=H)),
                    reads=["S_z"], writes=["S_vnew"])
            qa = S_z[0:N, 0:1024].rearrange("p (c f e) -> p c f e", c=8, f=2)
            ka = S_z[0:N, 1024:1152].rearrange("p (f e) -> p f e", f=2)
            pnA = S_pn[0:N, 0:1024].rearrange("p (c f e) -> p c f e", c=8, f=2)
            P.op("dve", lambda e: e.tensor_tensor(pnA, qa, ka[:, None, :, :].broadcast_to([N, 8, 2, 64]), op=ALU.mult),
                 reads=["S_z"], writes=["S_pn"])
            for g in range(3):
                base = 1280 + 768 * g
                P.op("dve", lambda e, g=g, base=base: e.tensor_tensor(
                    S_pn[0:N, 1024 + 256 * g:1024 + 256 * (g + 1)], S_z[0:N, base:base + 256],
                    S_z[0:N, base + 256:base + 512], op=ALU.mult), reads=["S_z"], writes=["S_pn"])
            P.op("dve", lambda e: e.tensor_reduce(S_pnew[0:N, :], S_pn[0:N, :].rearrange("p (h e) -> p h e", h=28),
                                                  axis=AX.X, op=ALU.add), reads=["S_pn"], writes=["S_pnew"])
            P.op("act", lambda e: e.activation(out=S_pnew[0:N, :], in_=S_pnew[0:N, :], func=AF.Exp, scale=0.125),
                 reads=["S_pnew"], writes=["S_pnew"])
            P.op("dve", lambda e: e.tensor_tensor(
                S_pd[0:N], S_pnew[0:N, :][:, None, :].broadcast_to([N, 16, 28]),
                S_eye[0:N, :][:, :, None].broadcast_to([N, 16, 28]), op=ALU.mult),
                reads=["S_pnew", "S_eye"], writes=["S_pd"])
            if SU_ == "pre":
                return
            OA, OB = 5, 6
            es_v = esrow[0:1, :].rearrange("p (h n) -> p h n", h=16)
            pend_pv = [None]

            def pv_row(b, q2):
                ptk = "S_pt%d" % q2
                for half in range(2):
                    oa_ap = PB(OA)[:, b * 16 + half * 8:b * 16 + half * 8 + 8]
                    P.op("pe", lambda e, oa_ap=oa_ap, q2=q2, half=half: e.matmul(
                        oa_ap, S_va[q2][0][:, half, :], S_pt[q2][:, half:16:2], start=True, stop=False),
                        reads=["S_va%d_0" % q2, ptk], writes=[PK(OA)])
                    P.op("pe", lambda e, oa_ap=oa_ap, b=b, half=half: e.matmul(
                        oa_ap, S_vnew[0:16, half, :], S_pd[0:16, b, half:16:2], start=False, stop=False),
                        reads=["S_vnew", "S_pd"], writes=[PK(OA)])
                    P.op("pe", lambda e, oa_ap=oa_ap, half=half: e.matmul(
                        oa_ap, sinkl[0:1, :], es_v[:, half * 8:half * 8 + 8, 0], start=False, stop=True),
                        reads=["sinkl", "esrow"], writes=[PK(OA)])
                for h in range(4):
                    ob_ap = PB(OB)[:, b * 4 + h:b * 4 + h + 1]
                    for g in range(3):
                        P.op("pe", lambda e, ob_ap=ob_ap, q2=q2, h=h, g=g: e.matmul(
                            ob_ap, S_va[q2][1 + g][:, h, :], S_pt[q2][:, 16 + 4 * g + h:16 + 4 * g + h + 1],
                            start=(g == 0), stop=False), reads=["S_va%d_%d" % (q2, 1 + g), ptk], writes=[PK(OB)])
                        P.op("pe", lambda e, ob_ap=ob_ap, b=b, h=h, g=g: e.matmul(
                            ob_ap, S_vnew[0:16, 2 + 4 * g + h, :], S_pd[0:16, b, 16 + 4 * g + h:16 + 4 * g + h + 1],
                            start=False, stop=(g == 2)), reads=["S_vnew", "S_pd"], writes=[PK(OB)])

            for b in range(N):
                q2 = b % 2
                for (bk, c0, n, o0) in [(0, 0, 512, 0), (1, 512, 512, 0), (2, 1280, 256, 0), (2, 1280 + 768, 256, 256),
                                        (3, 1280 + 1536, 256, 0)]:
                    P.op("pe", lambda e, b=b, bk=bk, c0=c0, n=n, o0=o0: e.matmul(
                        PB(bk)[:, o0:o0 + n], S_sel[0:16, b, :], S_zq[0:16, c0:c0 + n], start=True, stop=True),
                        reads=["S_sel", "S_zq"], writes=[PK(bk)])
                if pend_pv[0] is not None:
                    pend_pv[0]()
                    pend_pv[0] = None
                pts = []
                for si in range(4):
                    H = 2 if si == 0 else 4
                    src = ca_d if si == 0 else cb_d[si - 1]
                    dd = [1, 1, 4, 16][si]
                    kvt = S_kv[q2][si]
                    kvk = "S_kv%d_%d" % (q2, si)
                    dma("sp", kvt, src[b, 0:128 * dd:dd].rearrange("j k h e -> j (k h e)"), [], [kvk])
                    P.op("act", lambda e, kvt=kvt, H=H, q2=q2, si=si: e.copy(
                        S_va[q2][si][:, 0:H, 0:64], kvt[:, H * 64:2 * H * 64].rearrange("p (h e) -> p h e", h=H)),
                        reads=[kvk], writes=["S_va%d_%d" % (q2, si)])
                    if si == 0:
                        kview = kvt[:, 0:128].rearrange("p (f e) -> p f e", f=2)[:, None, :, :].broadcast_to([128, 8, 2, 64])
                        qv = PB(0).rearrange("p (c f e) -> p c f e", c=4, f=2)
                        pv_ = S_prod[:, 0:1024].rearrange("p (c f e) -> p c f e", c=8, f=2)
                        P.op("dve", lambda e, kview=kview, pv_=pv_: e.tensor_tensor(
                            pv_[:, 0:4], PB(0).rearrange("p (c f e) -> p c f e", c=4, f=2), kview[:, 0:4], op=ALU.mult),
                            reads=[PK(0), kvk], writes=["S_prod"])
                        P.op("dve", lambda e, kview=kview, pv_=pv_: e.tensor_tensor(
                            pv_[:, 4:8], PB(1).rearrange("p (c f e) -> p c f e", c=4, f=2), kview[:, 4:8], op=ALU.mult),
                            reads=[PK(1), kvk], writes=["S_prod"])
                        nh = 16
                    else:
                        bk, o0 = [(2, 0), (2, 256), (3, 0)][si - 1]
                        P.op("dve", lambda e, kvt=kvt, bk=bk, o0=o0: e.tensor_tensor(
                            S_prod[:, 0:256], PB(bk)[:, o0:o0 + 256], kvt[:, 0:256], op=ALU.mult),
                            reads=[PK(bk), kvk], writes=["S_prod"])
                        nh = 4
                    P.op("dve", lambda e, nh=nh: e.tensor_reduce(
                        S_st[:, 0:nh], S_prod[:, 0:nh * 64].rearrange("p (h e) -> p h e", h=nh), axis=AX.X, op=ALU.add),
                        reads=["S_prod"], writes=["S_st"])
                    po0 = 0 if si == 0 else 16 + 4 * (si - 1)
                    P.op("act", lambda e, nh=nh, q2=q2, po0=po0: e.activation(
                        out=S_pt[q2][:, po0:po0 + nh], in_=S_st[:, 0:nh], func=AF.Exp, scale=0.125),
                        reads=["S_st"], writes=["S_pt%d" % q2])
                pend_pv[0] = (lambda b=b, q2=q2: pv_row(b, q2))
            if pend_pv[0] is not None:
                pend_pv[0]()
                pend_pv[0] = None
            if SU_ == "attn":
                return
            P.op("dve", lambda e: e.reciprocal(S_rden[0:64, 0:256], PB(OA)[64:128, 0:256]), reads=[PK(OA)], writes=["S_rden"])
            for half in range(2):
                P.op("dve", lambda e, half=half: e.tensor_tensor(
                    S_oaT[half * 64:half * 64 + 64, :, 0:N].rearrange("p c b -> p b c"),
                    PB(OA)[0:64, 0:256].rearrange("p (b f c) -> p b f c", b=16, f=2)[:, 0:N, half, :],
                    S_rden[0:64, 0:256].rearrange("p (b f c) -> p b f c", b=16, f=2)[:, 0:N, half, :], op=ALU.mult),
                    reads=[PK(OA), "S_rden"], writes=["S_oaT"])
            P.op("dve", lambda e: e.reciprocal(S_rden[0:64, 0:64], PB(OB)[64:128, 0:64]), reads=[PK(OB)], writes=["S_rden"])
            for half in range(2):
                P.op("dve", lambda e, half=half: e.tensor_tensor(
                    S_obT[half * 64:half * 64 + 64, :, 0:N].rearrange("p c b -> p b c"),
                    PB(OB)[0:64, 0:64].rearrange("p (b c f) -> p b c f", b=16, c=2)[:, 0:N, :, half],
                    S_rden[0:64, 0:64].rearrange("p (b c f) -> p b c f", b=16, c=2)[:, 0:N, :, half], op=ALU.mult),
                    reads=[PK(OB), "S_rden"], writes=["S_obT"])
            if SU_ == "norm":
                return
            def ev_pa(bk, c0, n):
                P.op("dve", lambda e: e.tensor_tensor(S_pa[0:N, c0:c0 + n], PB(bk)[0:N, 0:n], S_g[0:N, c0:c0 + n], op=ALU.mult),
                     reads=[PK(bk), "S_g"], writes=["S_pa"])
            tm_matmul(S_oaT, "S_oaT", 8, w_pa_b, "w_pa_b", 0, 0, 1024, ev_pa)

            def ev_pb(bk, c0, n):
                P.op("dve", lambda e: e.tensor_tensor(S_pb[0:N, c0:c0 + n], PB(bk)[0:N, 0:n], S_g[0:N, 1024 + c0:1024 + c0 + n], op=ALU.mult),
                     reads=[PK(bk), "S_g"], writes=["S_pb"])
            tm_matmul(S_obT, "S_obT", 2, w_pb_b, "w_pb_b", 0, 0, 1024, ev_pb)
            P.op("dve", lambda e: e.tensor_tensor(S_mbf[0:N, :], S_pa[0:N, :], S_pb[0:N, :], op=ALU.add),
                 reads=["S_pa", "S_pb"], writes=["S_mbf"])
            transpose_to(S_mbf[0:N, :], "S_mbf", S_mT, "S_mT", 0, npart=N)

            if SU_ == "proj":
                return

            def ev_x1(bk, c0, n):
                P.op("dve", lambda e: e.tensor_tensor(S_xs[0:N, c0:c0 + n], PB(bk)[0:N, 0:n], S_xs[0:N, c0:c0 + n], op=ALU.add),
                     reads=[PK(bk), "S_xs"], writes=["S_xs"])
            tm_matmul(S_mT, "S_mT", 8, w_o_b, "w_o_b", 0, 0, 1024, ev_x1)
            if SU_ == "wo":
                return
            rms_norm(S_xs[0:N, :], ["S_xs"], g2b, "g2b", S_hn[0:N, :], ["S_hn"], npart=N)
            transpose_to(S_hn[0:N, :], "S_hn", S_hT, "S_hT", 0, npart=N)

            def ev_gu(bk, c0, n):
                P.op("act", lambda e: e.copy(S_z[0:N, c0:c0 + n], PB(bk)[0:N, 0:n]), reads=[PK(bk)], writes=["S_z"])
            tm_matmul(S_hT, "S_hT", 8, w_gu_b, "w_gu_b", 0, 0, 2 * D_FF, ev_gu)
            if SU_ == "gu":
                return
            P.op("act", lambda e: e.activation(out=S_z[0:N, 0:D_FF], in_=S_z[0:N, 0:D_FF], func=AF.Silu),
                 reads=["S_z"], writes=["S_z"])
            P.op("dve", lambda e: e.tensor_tensor(S_abf[0:N, :], S_z[0:N, 0:D_FF], S_z[0:N, D_FF:2 * D_FF], op=ALU.mult),
                 reads=["S_z"], writes=["S_abf"])
            pvb = PB(4).bitcast(BF16).rearrange("p (c t) -> p c t", c=32)
            for fc in range(22):
                P.op("pe", lambda e, fc=fc: e.transpose(pvb[:, fc, 0:N], S_abf[0:N, fc * 128:(fc + 1) * 128],
                                                        ident_bf[0:N, 0:N]),
                     reads=["S_abf", "ident_bf"], writes=[PK(4)])
            P.op("act", lambda e: e.copy(S_aT[:, :, 0:N], pvb[:, 0:22, 0:N]), reads=[PK(4)], writes=["S_aT"])

            if SU_ == "a":
                return

            def ev_dn(bk, c0, n):
                P.op("dve", lambda e: e.tensor_tensor(S_xs[0:N, c0:c0 + n], PB(bk)[0:N, 0:n], S_xs[0:N, c0:c0 + n], op=ALU.add),
                     reads=[PK(bk), "S_xs"], writes=["S_xs"])
            tm_matmul(S_aT, "S_aT", 22, w_dn_b, "w_dn_b", 0, 0, 1024, ev_dn)
            rms_norm(S_xs[0:N, :], ["S_xs"], gfb, "gfb", S_y[0:N, :], ["S_y"], npart=N)
            dma("pool", ys_o, S_y[0:N, :], ["S_y"], ["ys_out"], final=True)

        if sets is not None:
            SETS[:] = [st for st in SETS if st["name"] in sets]
        import os
        SP_ = os.environ.get("SAMPLE_PARTS", "cp")
        SU_ = os.environ.get("S_UPTO", "")
        if do_sample and "c" in SP_:
            sample_copies()
        def mark(name):
            P.marks.append((name, sum(1 for o in P.ops if o["eng"] == "pe")))
        P.marks = []
        if do_prompt:
            for s in range(n_seq):
                if "1" in stages:
                    mark("s%d stage1" % s)
                    stage1(s)
                    P.barrier()
                if s == 0:
                    cast_rest()
                if "2" in stages:
                    mark("s%d stage2" % s)
                    stage2(s)
                    P.barrier()
                if "3" in stages:
                    mark("s%d stage3" % s)
                    stage3(s)
                    P.barrier()
        if "w_pa_b" not in wkeys:
            cast_rest()
        mark("sample")
        issue_bulk(len(bulk_list))
        if do_sample and "p" in SP_:
            sample_phase()
            P.barrier()
        if dbg is not None:
            dbg_o = dram("dbg", [128, 8 * SEQ], BF16, OUT)
            src = {"hT": (hT, "hT"), "oaT": (oaT, "oaT")}[dbg]
            dma("sp", dbg_o, src[0].rearrange("p c t -> p (c t)"), [src[1]], ["dbg"], final=True)
        P.emit()
    return nc, P


def host_consts():
    bf = ml_dtypes.bfloat16
    ident = np.eye(128, dtype=np.float32)
    k = np.arange(128)[:, None]
    q = np.arange(128)[None, :]
    mask = np.zeros((128, 512), np.float32)
    mask[:, 0:128] = np.where(k >= q, 0.0, NEG)
    mask[:, 128:256] = np.where(k <= q, 0.0, NEG)
    mask[:, 256:512] = mask[:, 0:256]
    rot = np.zeros((128, 128), np.float32)
    for m in range(128):
        if (m % 64) < 32:
            rot[m + 32, m] = -1.0
        else:
            rot[m - 32, m] = 1.0
    half = 32
    inv = (1.0 / (10000.0 ** (np.arange(half, dtype=np.float32) / half))).astype(np.float32)
    pos = np.arange(SEQ, dtype=np.float32)
    ang = pos[None, :] * inv[(np.arange(128) % 64) % 32][:, None]
    ang = ang.astype(np.float32)
    sinkl = np.zeros((1, 128), np.float32)
    sinkl[0, 64:] = 1.0
    sel = np.zeros((16, 16, 128), np.float32)
    for b in range(16):
        sel[b, b, :] = 1.0
    ang_s = (np.float32(PAST_LEN) * inv).astype(np.float32)
    ang_s = np.concatenate([ang_s, ang_s])
    cs = np.zeros((16, 128), np.float32)
    cs[:, 0:64] = np.cos(ang_s)[None, :]
    ssin = np.sin(ang_s).astype(np.float32)
    ssin[0:32] = -ssin[0:32]
    cs[:, 64:128] = ssin[None, :]
    return dict(c_ident_bf=ident.astype(bf), c_ident_f=ident, c_mask=mask.astype(bf),
                c_rot=rot.astype(bf), c_cos=np.cos(ang).astype(np.float32),
                c_sin=np.sin(ang).astype(np.float32), c_sinkl=sinkl.astype(bf),
                c_sel=sel.reshape(16, 16 * 128).astype(bf), c_eye16=np.eye(16, dtype=np.float32), c_cs=cs)


def host_weights(w_in, w_pa):
    w_in = np.asarray(w_in)[0]
    w_pa = np.asarray(w_pa)[0]
    qa = w_in[:, 0:1024].reshape(1024, 16, 64)
    perm_heads = []
    for c in range(8):
        perm_heads += [c, 8 + c]
    qa_p = qa[:, perm_heads, :].reshape(1024, 1024)
    ka = w_in[:, 1024:1152]
    va = w_in[:, 1152:1280]
    o3 = 1280
    qb = w_in[:, o3:o3 + 768].reshape(1024, 3, 256)
    kb = w_in[:, o3 + 768:o3 + 1536].reshape(1024, 3, 256)
    vb = w_in[:, o3 + 1536:o3 + 2304].reshape(1024, 3, 256)
    gates = w_in[:, o3 + 2304:]
    cols = [qa_p, ka, va]
    for g in range(3):
        cols += [qb[:, g], kb[:, g], vb[:, g]]
    cols.append(gates)
    w_in_dev = np.ascontiguousarray(np.concatenate(cols, axis=1), dtype=np.float32)
    assert w_in_dev.shape == (1024, N_IN)
    w_pa_dev = np.ascontiguousarray(w_pa.reshape(16, 64, 1024)[perm_heads].reshape(1024, 1024))
    return w_in_dev, w_pa_dev


_CACHE = {}


def kernel(x_prompt, x_sample, cache_a_kv, cache_b1_kv, cache_b2_kv, cache_b3_kv, norm1_g, w_in, sinks,
           w_pa, w_pb, w_o, norm2_g, w_gu, w_down, final_norm_g):
    n_seq = 2
    if "nc" not in _CACHE:
        _CACHE["nc"] = build_program(n_seq=n_seq)
    nc, _ = _CACHE["nc"]
    consts = host_consts()
    w_in_dev, w_pa_dev = host_weights(w_in, w_pa)
    shared = dict(consts)
    shared.update(w_in=w_in_dev, w_pa=w_pa_dev, w_pb=np.ascontiguousarray(np.asarray(w_pb)[0]),
                  w_o=np.ascontiguousarray(np.asarray(w_o)[0]), w_gu=np.ascontiguousarray(np.asarray(w_gu)[0]),
                  w_dn=np.ascontiguousarray(np.asarray(w_down)[0]),
                  g1=np.asarray(norm1_g).reshape(1, 1024), g2=np.asarray(norm2_g).reshape(1, 1024),
                  gf=np.asarray(final_norm_g).reshape(1, 1024), sinks=np.asarray(sinks).reshape(1, 16))
    x_prompt = np.asarray(x_prompt)
    x_sample = np.asarray(x_sample).reshape(128, D_MODEL)
    ca = np.asarray(cache_a_kv)[0]
    cbs = [np.asarray(cache_b1_kv)[0], np.asarray(cache_b2_kv)[0], np.asarray(cache_b3_kv)[0]]
    in_maps = []
    for c in range(N_CORES):
        m = dict(shared)
        m["xp"] = np.ascontiguousarray(x_prompt[c * n_seq:(c + 1) * n_seq])
        sl = slice(c * 16, (c + 1) * 16)
        m["xs"] = np.ascontiguousarray(x_sample[sl])
        m["ca"] = np.ascontiguousarray(ca[sl])
        for i in range(3):
            m["cb%d" % (i + 1)] = np.ascontiguousarray(cbs[i][sl])
        in_maps.append(m)
    res = run_bass_kernel_spmd(nc, in_maps, core_ids=list(range(N_CORES)))
    R = res.results
    y_prompt = np.concatenate([r["yp"] for r in R], axis=0)
    pa = np.concatenate([r["pa"] for r in R], axis=0)[None]
    pb1 = np.concatenate([r["pb1"] for r in R], axis=0)[None]
    pb2 = np.concatenate([r["pb2"] for r in R], axis=0)[None]
    pb3 = np.concatenate([r["pb3"] for r in R], axis=0)[None]
    y_sample = np.concatenate([r["ys"] for r in R], axis=0).reshape(128, 1, D_MODEL)
    sa = np.concatenate([r["sa"] for r in R], axis=0)[None]
    sb1 = np.concatenate([r["sb1"] for r in R], axis=0)[None]
    sb2 = np.concatenate([r["sb2"] for r in R], axis=0)[None]
    sb3 = np.concatenate([r["sb3"] for r in R], axis=0)[None]
    return (y_prompt, y_sample, pa, pb1, pb2, pb3, sa, sb1, sb2, sb3)
```
